# Optimizing a Trainium2 kernel written in Bass

```python
import jax, jax.numpy as jnp
from jax import lax
import numpy as np

D_MODEL = 1024
BATCH = 16
SEQ = 2048
DEPTH = 1

CHUNK = 64
Q_BLOCK = 128
EPS = 1e-6
GLA_HEADS = 4
GLA_DK = 128
GLA_DV = 256
GLA_LOWRANK = 16
GLA_TAU = 16.0
MLA_HEADS = 16
MLA_Q_RANK = 256
MLA_KV_RANK = 128
MLA_NOPE = 64
MLA_ROPE = 32
MLA_V = 64
ROPE_THETA = 10000.0
D_FF = 4 * D_MODEL
N_BRANCH = 2
IN_SPLITS = (GLA_HEADS * GLA_DK, GLA_HEADS * GLA_DK, GLA_HEADS * GLA_DV, GLA_HEADS * GLA_DV,
             GLA_LOWRANK, MLA_Q_RANK, MLA_KV_RANK, MLA_ROPE, N_BRANCH * D_MODEL)
IN_WIDTH = sum(IN_SPLITS)

kernel_name = "hybrid_gla_mla_sqrelu_adaln_block"


def rms_norm(x, g):
    xf = x.astype(jnp.float32)
    y = xf * lax.rsqrt(jnp.mean(xf * xf, axis=-1, keepdims=True) + EPS)
    return (y * g.astype(jnp.float32)).astype(x.dtype)


def modulate(h, shift, scale):
    return h * (1.0 + scale[:, None, :]) + shift[:, None, :]


def rope(x, positions):
    r = x.shape[-1]
    freqs = ROPE_THETA ** (-jnp.arange(0, r, 2, dtype=jnp.float32) / r)
    ang = positions.astype(jnp.float32)[..., None] * freqs
    cos = jnp.cos(ang)[:, :, None, :]
    sin = jnp.sin(ang)[:, :, None, :]
    xf = x.astype(jnp.float32)
    x1, x2 = xf[..., : r // 2], xf[..., r // 2:]
    return jnp.concatenate([x1 * cos - x2 * sin, x2 * cos + x1 * sin], axis=-1).astype(x.dtype)


def gla_branch(q, k, v, g, a_lr, w_alpha, b_alpha, out_norm_g, w_o):
    b, s, _ = q.shape
    nc = s // CHUNK
    qc = q.reshape(b, nc, CHUNK, GLA_HEADS, GLA_DK) * (GLA_DK ** -0.5)
    kc = k.reshape(b, nc, CHUNK, GLA_HEADS, GLA_DK)
    vc = v.reshape(b, nc, CHUNK, GLA_HEADS, GLA_DV)
    log_a = jax.nn.log_sigmoid((a_lr @ w_alpha + b_alpha).astype(jnp.float32)) / GLA_TAU
    log_a = log_a.reshape(b, nc, CHUNK, GLA_HEADS, GLA_DK)
    cum = jnp.cumsum(log_a, axis=2)
    cum_end = cum[:, :, -1]
    k_dec = kc.astype(jnp.float32) * jnp.exp(cum_end[:, :, None] - cum)
    u = jnp.einsum('bnchk,bnchv->nbhkv', k_dec, vc.astype(jnp.float32))
    decay = jnp.transpose(jnp.exp(cum_end), (1, 0, 2, 3))

    def step(state, inp):
        d, uc = inp
        state = d[..., None] * state + uc
        return state, state

    s0 = jnp.zeros((b, GLA_HEADS, GLA_DK, GLA_DV), jnp.float32)
    _, states = lax.scan(step, s0, (decay, u))
    o = jnp.einsum('bnchk,nbhkv->bnchv', qc.astype(jnp.float32), states).astype(q.dtype)
    o = o.reshape(b, s, GLA_HEADS, GLA_DV)
    o = rms_norm(o, out_norm_g) * jax.nn.silu(g.reshape(b, s, GLA_HEADS, GLA_DV))
    return o.reshape(b, s, GLA_HEADS * GLA_DV) @ w_o


def chunk_causal_attention(q, k, v):
    b, s, h, dqk = q.shape
    dv = v.shape[-1]
    nb = s // Q_BLOCK
    scale = dqk ** -0.5
    qb = jnp.transpose(q.reshape(b, nb, Q_BLOCK, h, dqk), (1, 0, 3, 2, 4))
    key_chunk = jnp.arange(s) // CHUNK

    def one_block(args):
        qi, bi = args
        sc = jnp.einsum('bhqd,bkhd->bhqk', qi, k).astype(jnp.float32) * scale
        q_chunk = (bi * Q_BLOCK + jnp.arange(Q_BLOCK)) // CHUNK
        mask = key_chunk[None, :] <= q_chunk[:, None]
        sc = jnp.where(mask[None, None], sc, -jnp.inf)
        p = jax.nn.softmax(sc, axis=-1).astype(v.dtype)
        return jnp.einsum('bhqk,bkhd->bqhd', p, v)

    out = lax.map(one_block, (qb, jnp.arange(nb)))
    return jnp.transpose(out, (1, 0, 2, 3, 4)).reshape(b, s, h, dv)


def mla_branch(cq, ckv, kpe, positions, q_lat_g, w_uq, kv_lat_g, w_ukv, qn_g, kn_g, w_o):
    b, s, _ = cq.shape
    q = (rms_norm(cq, q_lat_g) @ w_uq).reshape(b, s, MLA_HEADS, MLA_NOPE + MLA_ROPE)
    kv = (rms_norm(ckv, kv_lat_g) @ w_ukv).reshape(b, s, MLA_HEADS, MLA_NOPE + MLA_V)
    k_nope, v = kv[..., :MLA_NOPE], kv[..., MLA_NOPE:]
    k_rope = jnp.broadcast_to(kpe[:, :, None, :], (b, s, MLA_HEADS, MLA_ROPE))
    k = jnp.concatenate([k_nope, k_rope], axis=-1)
    q = rms_norm(q, qn_g)
    k = rms_norm(k, kn_g)
    q = jnp.concatenate([q[..., :MLA_NOPE], rope(q[..., MLA_NOPE:], positions)], axis=-1)
    k = jnp.concatenate([k[..., :MLA_NOPE], rope(k[..., MLA_NOPE:], positions)], axis=-1)
    o = chunk_causal_attention(q, k, v)
    return o.reshape(b, s, MLA_HEADS * MLA_V) @ w_o


def setup_inputs(seed: int = 0) -> dict:
    key = jax.random.key(seed)
    ks = jax.random.split(key, 24)
    f32 = jnp.float32

    def nrm(k, shape, scale):
        return jax.random.normal(k, shape, f32) * scale

    def gain(k, dim):
        return 1.0 + 0.02 * jax.random.normal(k, (DEPTH, dim), f32)

    L = DEPTH
    offsets = jax.random.randint(ks[2], (BATCH, 1), 0, 4096, dtype=jnp.int32)
    positions = offsets + jnp.arange(SEQ, dtype=jnp.int32)[None, :]
    return {
        "x": nrm(ks[0], (BATCH, SEQ, D_MODEL), 1.0),
        "c": nrm(ks[1], (BATCH, D_MODEL), 1.0),
        "positions": positions,
        "w_ada": nrm(ks[3], (L, D_MODEL, 6 * D_MODEL), 0.5 * D_MODEL ** -0.5),
        "b_ada": nrm(ks[4], (L, 6 * D_MODEL), 0.02),
        "norm1_g": gain(ks[5], D_MODEL),
        "w_in": nrm(ks[6], (L, D_MODEL, IN_WIDTH), D_MODEL ** -0.5),
        "b_merge": nrm(ks[7], (L, N_BRANCH * D_MODEL), 0.02),
        "gla_w_alpha": nrm(ks[8], (L, GLA_LOWRANK, GLA_HEADS * GLA_DK), GLA_LOWRANK ** -0.5),
        "gla_b_alpha": nrm(ks[9], (L, GLA_HEADS * GLA_DK), 0.1),
        "gla_out_norm_g": gain(ks[10], GLA_DV),
        "gla_w_o": nrm(ks[11], (L, GLA_HEADS * GLA_DV, D_MODEL), (GLA_HEADS * GLA_DV) ** -0.5),
        "mla_q_lat_g": gain(ks[12], MLA_Q_RANK),
        "mla_w_uq": nrm(ks[13], (L, MLA_Q_RANK, MLA_HEADS * (MLA_NOPE + MLA_ROPE)), MLA_Q_RANK ** -0.5),
        "mla_kv_lat_g": gain(ks[14], MLA_KV_RANK),
        "mla_w_ukv": nrm(ks[15], (L, MLA_KV_RANK, MLA_HEADS * (MLA_NOPE + MLA_V)), MLA_KV_RANK ** -0.5),
        "mla_qn_g": gain(ks[16], MLA_NOPE + MLA_ROPE),
        "mla_kn_g": gain(ks[17], MLA_NOPE + MLA_ROPE),
        "mla_w_o": nrm(ks[18], (L, MLA_HEADS * MLA_V, D_MODEL), (MLA_HEADS * MLA_V) ** -0.5),
        "w_out": nrm(ks[19], (L, D_MODEL, D_MODEL), D_MODEL ** -0.5),
        "norm2_g": gain(ks[20], D_MODEL),
        "mlp_w1": nrm(ks[21], (L, D_MODEL, D_FF), D_MODEL ** -0.5),
        "mlp_w2": nrm(ks[22], (L, D_FF, D_MODEL), D_FF ** -0.5),
    }


def reference(x, c, positions, w_ada, b_ada, norm1_g, w_in, b_merge, gla_w_alpha, gla_b_alpha,
              gla_out_norm_g, gla_w_o, mla_q_lat_g, mla_w_uq, mla_kv_lat_g, mla_w_ukv,
              mla_qn_g, mla_kn_g, mla_w_o, w_out, norm2_g, mlp_w1, mlp_w2):
    split_at = np.cumsum(IN_SPLITS)[:-1].tolist()
    c_act = jax.nn.silu(c)
    for l in range(DEPTH):
        mod = c_act @ w_ada[l] + b_ada[l]
        shift1, scale1, gate1, shift2, scale2, gate2 = jnp.split(mod, 6, axis=-1)

        h = modulate(rms_norm(x, norm1_g[l]), shift1, scale1)
        proj = h @ w_in[l]
        g_q, g_k, g_v, g_g, g_a, m_cq, m_ckv, m_kpe, merge_logits = jnp.split(proj, split_at, axis=-1)
        y_a = gla_branch(g_q, g_k, g_v, g_g, g_a, gla_w_alpha[l], gla_b_alpha[l],
                         gla_out_norm_g[l], gla_w_o[l])
        y_b = mla_branch(m_cq, m_ckv, m_kpe, positions, mla_q_lat_g[l], mla_w_uq[l],
                         mla_kv_lat_g[l], mla_w_ukv[l], mla_qn_g[l], mla_kn_g[l], mla_w_o[l])
        gates = jax.nn.sigmoid(merge_logits + b_merge[l])
        gate_a, gate_b = gates[..., :D_MODEL], gates[..., D_MODEL:]
        mixed = (gate_a * y_a + gate_b * y_b) @ w_out[l]
        x = x + gate1[:, None, :] * mixed

        h2 = modulate(rms_norm(x, norm2_g[l]), shift2, scale2)
        ff = jnp.square(jax.nn.relu(h2 @ mlp_w1[l])) @ mlp_w2[l]
        x = x + gate2[:, None, :] * ff
    return x
```

```python
import numpy as np
from contextlib import ExitStack
import concourse.bass as bass
import concourse.mybir as mybir
from concourse.bass_utils import run_bass_kernel_spmd

F32 = mybir.dt.float32
BF16 = mybir.dt.bfloat16
I32 = mybir.dt.int32
AF = mybir.ActivationFunctionType
ALU = mybir.AluOpType
AX = mybir.AxisListType

D = 1024
SEQ = 2048
NB = 2
TB = 512
NTT = TB // 128
NBLK = SEQ // TB
EPS = 1e-6
INW = 5552
DFF = 4096
SLOT = 2304
NSLOT = 4
PF = 2
TWO_PI = float(2 * np.pi)
NCST = 739


class Buf:
    __slots__ = ("name", "w", "r")

    def __init__(self, name):
        self.name = name
        self.w = {}
        self.r = {}


class Eng:
    def __init__(self, fw, eng, name, is_pe=False):
        self.e = eng
        self.name = name
        self.is_pe = is_pe
        self.sem = fw.new_sem("s_" + name)
        self.cnt = 0
        self.waited = {}


class FW:
    def __init__(self, nc, stack):
        self.nc = nc
        self.stack = stack
        self.pe = Eng(self, nc.tensor, "pe", is_pe=True)
        self.act = Eng(self, nc.scalar, "act")
        self.dve = Eng(self, nc.vector, "dve")
        self.pool = Eng(self, nc.gpsimd, "pool")
        self.sp = Eng(self, nc.sync, "sp")
        self.engs = [self.pe, self.act, self.dve, self.pool, self.sp]

    def new_sem(self, name):
        return self.stack.enter_context(self.nc.semaphore(name))

    def sbuf(self, name, shape, dt):
        return self.stack.enter_context(self.nc.sbuf_tensor("sb_" + name, list(shape), dt))

    def psum(self, name, shape, dt):
        return self.stack.enter_context(self.nc.psum_tensor("ps_" + name, list(shape), dt))

    def _collect(self, E, reads, writes):
        need = {}

        def add(k, sem, c):
            if k not in need or need[k][1] < c:
                need[k] = (sem, c)

        for b in reads:
            for k, (sem, c) in b.w.items():
                add(k, sem, c)
        for b in writes:
            for k, (sem, c) in b.w.items():
                add(k, sem, c)
            for k, (sem, c) in b.r.items():
                add(k, sem, c)
        if E.is_pe:
            need.pop(E.name, None)
        return need

    def _do_waits(self, E, need):
        for k, (sem, c) in need.items():
            if E.waited.get(k, 0) >= c:
                continue
            E.e.wait_ge(sem, c)
            E.waited[k] = c

    def op(self, E, fn, reads=(), writes=()):
        self._do_waits(E, self._collect(E, reads, writes))
        ins = fn(E.e)
        E.cnt += 1
        ins.then_inc(E.sem, 1)
        for b in reads:
            b.r[E.name] = (E.sem, E.cnt)
        for b in writes:
            b.w = {E.name: (E.sem, E.cnt)}
            b.r = {}
        return ins

    def new_dsem(self, name):
        return [self.new_sem(name), 0, "d_" + name]

    def dma(self, Q, ds, out_ap, in_ap, reads=(), writes=(), **kw):
        self._do_waits(Q, self._collect(Q, reads, writes))
        ins = Q.e.dma_start(out=out_ap, in_=in_ap, **kw)
        ds[1] += 16
        ins.then_inc(ds[0], 16)
        for b in reads:
            b.r[ds[2]] = (ds[0], ds[1])
        for b in writes:
            b.w = {ds[2]: (ds[0], ds[1])}
            b.r = {}
        return ins

    def wait_all(self, E, bufs):
        need = {}
        for b in bufs:
            for d in (b.w, b.r):
                for k, (sem, c) in d.items():
                    if k not in need or need[k][1] < c:
                        need[k] = (sem, c)
        self._do_waits(E, need)


def host_consts():
    c = np.zeros((128, NCST), np.float32)
    c[:, 0:128] = np.eye(128, dtype=np.float32)
    s = np.arange(128)[:, None]
    t = np.arange(128)[None, :]
    c[:, 128:256] = ((s // 64 == t // 64) & (s > t)).astype(np.float32)
    c[:, 256:258] = (s // 64 == np.arange(2)[None, :]).astype(np.float32)
    c[:, 258:386] = 1.0
    perm = np.zeros((128, 96), np.float32)
    for m in range(64, 80):
        perm[m + 16, m] = -1.0
    for m in range(80, 96):
        perm[m - 16, m] = 1.0
    c[:, 386:482] = perm
    fr = (np.float32(10000.0) ** (-np.arange(0, 32, 2, dtype=np.float32) / np.float32(32))).astype(np.float32)
    c[64:80, 482] = fr
    c[80:96, 482] = fr
    for b in range(2):
        c[b, 483 + b * 128: 483 + (b + 1) * 128] = 1.0
    return c


def build(max_blocks=None, nseq=NB, debug=False):
    nc = bass.Bass("TRN2", target_bir_lowering=False)

    def din(name, shape, dt=F32):
        return nc.dram_tensor(name, list(shape), dt, kind="ExternalInput").ap()

    x_d = din("x", [NB, SEQ, D])
    c_d = din("c", [NB, D])
    pos_d = din("positions", [NB, SEQ], I32)
    wada_d = din("w_ada", [D, 6 * D])
    bada_d = din("b_ada", [1, 6 * D])
    n1g_d = din("norm1_g", [1, D])
    win_d = din("w_in", [D, INW])
    bmg_d = din("b_merge", [1, 2 * D])
    wal_d = din("gla_w_alpha", [16, 512])
    bal_d = din("gla_b_alpha", [1, 512])
    gng_d = din("gla_out_norm_g", [1, 256])
    wgo_d = din("gla_w_o", [D, D])
    qlg_d = din("mla_q_lat_g", [1, 256])
    wuq_d = din("mla_w_uq", [256, 1536])
    kvg_d = din("mla_kv_lat_g", [1, 128])
    wukv_d = din("mla_w_ukv", [128, 2048])
    qng_d = din("mla_qn_g", [1, 96])
    kng_d = din("mla_kn_g", [1, 96])
    wmo_d = din("mla_w_o", [D, D])
    wout_d = din("w_out", [D, D])
    n2g_d = din("norm2_g", [1, D])
    w1_d = din("mlp_w1", [D, DFF])
    w2_d = din("mlp_w2", [DFF, D])
    cst_d = din("cst", [128, NCST])
    out_d = nc.dram_tensor("out", [NB, SEQ, D], F32, kind="ExternalOutput").ap()
    dbg_d = {}
    if debug:
        for nm, shp in [("d_hT", [128, 8 * TB]), ("d_ogT", [128, 8 * TB]), ("d_OT", [64, 16 * TB]),
                        ("d_mixT", [128, 8 * TB]), ("d_x1", [128, NTT * D]), ("d_qT", [96, 16 * TB]),
                        ("d_rsk", [128, 16 * NTT])]:
            dbg_d[nm] = nc.dram_tensor(nm, shp, F32, kind="ExternalOutput").ap()

    def scr(name, shape):
        return nc.dram_tensor(name, list(shape), BF16, kind="Internal").ap()

    wada_s = scr("wada_s", [D, 6 * D])
    win_s = scr("win_s", [D, INW])
    wgo_s = scr("wgo_s", [D, D])
    wmo_s = scr("wmo_s", [D, D])
    wout_s = scr("wout_s", [D, D])
    w1_s = scr("w1_s", [D, DFF])
    w2_s = scr("w2_s", [DFF, D])
    wuq_s = scr("wuq_s", [256, 1536])
    wukv_s = scr("wukv_s", [128, 2048])

    with ExitStack() as st:
        fw = FW(nc, st)
        pe, act, dve, pool, sp = fw.pe, fw.act, fw.dve, fw.pool, fw.sp
        OP = fw.op

        wb = {}

        def convert(name, src, dst, rows, piece):
            b = Buf("scr_" + name)
            ds = fw.new_dsem("cv_" + name)
            for r0 in range(0, rows, piece):
                fw.dma(pool, ds, dst[r0:r0 + piece, :], src[r0:r0 + piece, :], writes=[b])
            wb[name] = b

        def convert_cols(name, src, dst, c0, c1, nsplit=2):
            b = Buf("scr_" + name)
            ds = fw.new_dsem("cv_" + name)
            rows = src.shape[0]
            step = rows // nsplit
            for r0 in range(0, rows, step):
                fw.dma(pool, ds, dst[r0:r0 + step, c0:c1], src[r0:r0 + step, c0:c1], writes=[b])
            wb[name] = b

        def do_conversions():
            convert("wukv", wukv_d, wukv_s, 128, 128)
            convert_cols("win_m", win_d, win_s, 3072, 3504)
            convert_cols("win_k", win_d, win_s, 512, 1024)
            convert_cols("win_v", win_d, win_s, 1024, 2048)
            convert_cols("win_q", win_d, win_s, 0, 512)
            convert_cols("win_g", win_d, win_s, 2048, 3072)
            convert("wuq", wuq_d, wuq_s, 256, 128)
            convert_cols("win_ma", win_d, win_s, 3504, 4528)
            convert_cols("win_mb", win_d, win_s, 4528, 5552)
            convert("wgo", wgo_d, wgo_s, D, 256)
            convert("wmo", wmo_d, wmo_s, D, 256)
            convert("wout", wout_d, wout_s, D, 256)
            convert("w1", w1_d, w1_s, D, 128)
            convert("w2", w2_d, w2_s, DFF, 512)

        assert TB == 512
        cst = fw.sbuf("cst", [128, NCST], F32)
        identb = fw.sbuf("identb", [128, 128], BF16)
        permb = fw.sbuf("permb", [128, 96], BF16)
        onesb = fw.sbuf("onesb", [128, 96], BF16)
        cols = fw.sbuf("cols", [128, 64], F32)
        modcol = fw.sbuf("modcol", [128, 4, 8, 2], F32)
        gcol = fw.sbuf("gcol", [128, 2, 8, 2], F32)
        balb = fw.sbuf("balb", [128, 512], F32)
        walb = fw.sbuf("walb", [16, 512], BF16)
        cT = fw.sbuf("cT", [128, 8, 2], F32)
        cact = fw.sbuf("cact", [128, 8, 2], F32)
        gate_bc = fw.sbuf("gate_bc", [128, 2, D], F32)
        rowt = fw.sbuf("rowt", [2, 256], F32)
        badar = fw.sbuf("badar", [2, 256], F32)
        cactb = fw.sbuf("cactb", [128, 8, 2], BF16)
        slots = [fw.sbuf(f"slot{i}", [128, SLOT], BF16) for i in range(NSLOT)]
        xt = fw.sbuf("xt", [128, NTT, D], F32)
        xn = fw.sbuf("xn", [128, D], BF16)
        junk = fw.sbuf("junk", [128, 256], BF16)
        hT = fw.sbuf("hT", [128, 8, TB], BF16)
        st4 = fw.sbuf("st4", [128, 16], F32)
        KC = fw.sbuf("KC", [128, SEQ], BF16)
        KR = fw.sbuf("KR", [128, SEQ], BF16)
        WkgT = fw.sbuf("WkgT", [64, 16, 128], BF16)
        wukv_sb = fw.sbuf("wukv_sb", [128, 2048], BF16)
        VAf = fw.sbuf("VA", [128, (SEQ // 128) * 16 * 65 + 64], BF16)
        VA = VAf[:, 0:(SEQ // 128) * 16 * 65].rearrange("p (k h d) -> p k h d", k=SEQ // 128, h=16)
        rsk = fw.sbuf("rsk", [128, SEQ // 128, 16], F32)
        S = fw.sbuf("S", [128, 4, 256], F32)
        Sbf = fw.sbuf("Sbf", [128, 2, 4, 256], BF16)
        spl = fw.sbuf("spl", [128, 512], F32)
        erev = fw.sbuf("erev", [128, 512], F32)
        dec = fw.sbuf("dec", [128, NTT, 4, 2], F32)
        alrT = fw.sbuf("alrT", [16, TB], BF16)
        latf = fw.sbuf("latf", [128, 3, TB], F32)
        latn = fw.sbuf("latn", [128, 2, TB], BF16)
        kpf = fw.sbuf("kpf", [96, TB], F32)
        kpq = fw.sbuf("kpq", [96, TB], F32)
        posi = fw.sbuf("posi", [96, TB], I32)
        ang = fw.sbuf("ang", [96, TB], F32)
        cosT = fw.sbuf("cosT", [96, TB], F32)
        sinT = fw.sbuf("sinT", [96, TB], F32)
        R1 = fw.sbuf("R1", [128, 32, TB], BF16)
        ogT = fw.sbuf("ogT", [128, 8, TB], BF16)
        og = fw.sbuf("og", [128, D], BF16)
        pts = [fw.sbuf(f"pt{i}", [128, TB], BF16) for i in range(3)]
        qas = [fw.sbuf(f"qa{i}", [128, TB], BF16) for i in range(2)]
        tmpf = [fw.sbuf(f"tmpf{i}", [128, TB], F32) for i in range(5)]
        ssk = fw.sbuf("ssk", [128, 20], F32)
        sqk = fw.sbuf("sqk", [128, 4, 64], F32)
        drow = fw.sbuf("drow", [65, TB], F32)
        qA = R1[:, 0:4, :]
        qB = R1[:, 4:8, :]
        kdec = R1[:, 8:12, :]
        gv = R1[:, 12:20, :].rearrange("p (t two) b -> p t (two b)", two=2)
        sg = R1[:, 20:28, :].rearrange("p (t two) b -> p t (two b)", two=2)
        adaf = R1[:, 28:32, :].rearrange("p a b -> p (a b)")
        adaslot = adaf.rearrange("p (k c) -> p k c", k=8)

        PG = [fw.psum(f"pg{i}", [128, 512], F32) for i in range(6)]
        PGb = [Buf(f"pg{i}") for i in range(6)]
        PT = [fw.psum(f"ptb{i}", [128, 8, 128], BF16) for i in range(2)]
        PTb = [Buf(f"ptb{i}") for i in range(2)]
        PTF = [t[:, :, :].rearrange("p a b -> p (a b)").bitcast(F32) for t in PT]
        rot = {"g": 0, "t": 0, "a": 0, "pt": 0, "tf": 0}

        def pg():
            i = rot["g"] % 4
            rot["g"] += 1
            return PG[i], PGb[i]

        def pa():
            i = 4 + rot["a"] % 2
            rot["a"] += 1
            return PG[i], PGb[i]

        def ptr():
            i = rot["t"] % 2
            rot["t"] += 1
            return PT[i], PTb[i]

        B = {}

        def bb(n):
            if n not in B:
                B[n] = Buf(n)
            return B[n]

        for i in range(3):
            bb(f"pt{i}")
        for i in range(5):
            bb(f"tmpf{i}")

        def ptile():
            i = rot["pt"] % 3
            rot["pt"] += 1
            return pts[i], B[f"pt{i}"]

        def qatile():
            i = rot.setdefault("qa", 0) % 2
            rot["qa"] += 1
            return qas[i], bb(f"qa{i}")

        def RB(lo, hi):
            return [bb(f"R1_{k}") for k in range(lo, hi)]

        def tf():
            i = rot["tf"] % 5
            rot["tf"] += 1
            return tmpf[i], B[f"tmpf{i}"]

        slotb = [Buf(f"slot{i}") for i in range(NSLOT)]
        slotds = [fw.new_dsem(f"slot{i}") for i in range(NSLOT)]
        ws = {"n": 0, "q": []}

        def ws_issue(view, src_buf):
            i = ws["n"] % NSLOT
            ws["n"] += 1
            P, A, C = view.shape
            dst = slots[i][0:P, 0:A * C].rearrange("p (a c) -> p a c", a=A)
            fw.dma(sp, slotds[i], dst, view, reads=[src_buf], writes=[slotb[i]])
            return dst, slotb[i]

        def kview(scr_ap, kc0, kc1, c0, c1, p=128):
            return scr_ap.rearrange("(kc p) n -> p kc n", p=p)[:, kc0:kc1, c0:c1]

        def block_sched():
            L = []
            L.append(("win_m", kview(win_s, 0, 8, 3072, 3344)))
            L.append(("win_m", kview(win_s, 0, 8, 3344, 3504)))
            for i in range(2, 4):
                L.append(("win_k", kview(win_s, 0, 8, 256 * i, 256 * i + 256)))
            for i in range(4, 8):
                L.append(("win_v", kview(win_s, 0, 8, 256 * i, 256 * i + 256)))
            for i in range(0, 2):
                L.append(("win_q", kview(win_s, 0, 8, 256 * i, 256 * i + 256)))
            for i in range(8, 12):
                L.append(("win_g", kview(win_s, 0, 8, 256 * i, 256 * i + 256)))
            L.append(("wuq", kview(wuq_s, 0, 2, 0, 768)))
            L.append(("wuq", kview(wuq_s, 0, 2, 768, 1536)))
            for cp in range(4):
                L.append(("win_ma", kview(win_s, 0, 8, 3504 + 256 * cp, 3504 + 256 * cp + 256)))
                L.append(("win_mb", kview(win_s, 0, 8, 3504 + 1024 + 256 * cp, 3504 + 1024 + 256 * cp + 256)))
                L.append(("wgo", kview(wgo_s, 0, 8, 256 * cp, 256 * cp + 256)))
                for i in range(2):
                    c = 2 * cp + i
                    L.append(("wmo", kview(wmo_s, 0, 16, 128 * c, 128 * c + 128, p=64)))
            for n in range(4):
                L.append(("wout", kview(wout_s, 0, 8, 256 * n, 256 * n + 256)))
            for n in range(16):
                L.append(("w1", kview(w1_s, 0, 8, 256 * n, 256 * n + 256)))
            for half in range(2):
                for n in range(8):
                    L.append(("w2", kview(w2_s, 4 * n, 4 * n + 4, 512 * half, 512 * half + 512)))
            return L

        SCHED = block_sched()
        total_blocks = nseq * (max_blocks if max_blocks else NBLK)
        wsp = {"req": 0, "iss": 0}

        def ws_get():
            lim = total_blocks * len(SCHED)
            while wsp["iss"] < min(wsp["req"] + PF + 1, lim):
                nm, v = SCHED[wsp["iss"] % len(SCHED)]
                ws["q"].append(ws_issue(v, wb[nm]))
                wsp["iss"] += 1
            wsp["req"] += 1
            return ws["q"].pop(0)

        cds = fw.new_dsem("cst")

        def small(dst, src, buf):
            fw.dma(sp, cds, dst, src, writes=[buf], allow_slow_non_contiguous=True)

        small(cst[:], cst_d, bb("cst"))
        small(cols[:, 0:8], n1g_d.rearrange("o (c p) -> p (o c)", p=128), bb("cols"))
        small(cols[:, 8:16], n2g_d.rearrange("o (c p) -> p (o c)", p=128), bb("cols"))
        small(cols[:, 16:32], bmg_d.rearrange("o (c p) -> p (o c)", p=128), bb("cols"))
        small(cols[:, 32:34], gng_d.rearrange("o (c p) -> p (o c)", p=128), bb("cols"))
        small(cols[:, 34:36], qlg_d.rearrange("o (c p) -> p (o c)", p=128), bb("cols"))
        small(cols[:, 36:37], kvg_d.rearrange("o (c p) -> p (o c)", p=128), bb("cols"))
        small(cols[0:96, 37:38], qng_d.rearrange("o (c p) -> p (o c)", p=96), bb("cols"))
        small(cols[0:96, 38:39], kng_d.rearrange("o (c p) -> p (o c)", p=96), bb("cols"))
        small(balb[:], bal_d.partition_broadcast(128), bb("balb"))
        for b_ in range(2):
            small(cT[:, :, b_], c_d[b_:b_ + 1, :].rearrange("o (c p) -> p (o c)", p=128), bb("cT"))
        for nm_ in ["cst", "cols", "balb", "cT"]:
            B[nm_].w = {cds[2]: (cds[0], cds[1])}
        wads = fw.new_dsem("wal")
        fw.dma(pool, wads, walb[:], wal_d, writes=[bb("walb")])

        identf = cst[:, 0:128]
        tri = cst[:, 128:256]
        ind = cst[:, 256:258]
        onesf = cst[:, 258:386]
        permf = cst[:, 386:482]
        freqc = cst[:, 482:483]
        CST = bb("cst")
        COLS = bb("cols")

        OP(dve, lambda e: e.tensor_copy(out=identb[:], in_=identf), reads=[CST], writes=[bb("identb")])
        OP(dve, lambda e: e.tensor_copy(out=permb[:], in_=permf), reads=[CST], writes=[bb("permb")])
        OP(dve, lambda e: e.tensor_copy(out=onesb[:], in_=onesf[:, 0:96]), reads=[CST], writes=[bb("onesb")])
        OP(dve, lambda e: e.tensor_scalar(out=cols[0:96, 39:40], in0=cols[0:96, 37:38], scalar1=96.0 ** -0.5, scalar2=None, op0=ALU.mult), reads=[COLS], writes=[COLS])
        OP(act, lambda e: e.activation(out=cact[:], in_=cT[:], func=AF.Silu), reads=[bb("cT")], writes=[bb("cact")])
        OP(dve, lambda e: e.tensor_copy(out=cactb[:], in_=cact[:]), reads=[bb("cact")], writes=[bb("cactb")])
        OP(pool, lambda e: e.memset(VAf[:], 1.0), writes=[bb("VA")])
        OP(pool, lambda e: e.memset(kpf[:], 0.0), writes=[bb("kpf")])
        OP(pool, lambda e: e.memset(KR[:], 0.0), writes=[bb("KR")])
        OP(pool, lambda e: e.memset(R1[:], 0.0), writes=RB(0, 32))
        OP(pool, lambda e: e.memset(cols[:, 40:41], EPS), reads=[COLS], writes=[COLS])
        do_conversions()
        epsc = cols[:, 40:41]
        wkds = fw.new_dsem("wkds")
        fw.dma(sp, wkds, wukv_sb[:], wukv_s, reads=[wb["wukv"]], writes=[bb("wukv_sb")])
        for hg in range(2):
            p, pb = ptr()
            for hh in range(8):
                h = 8 * hg + hh
                OP(pe, lambda e: e.transpose(out=p[0:64, hh, :], in_=wukv_sb[:, 128 * h:128 * h + 64], identity=identb[:]),
                   reads=[bb("wukv_sb"), bb("identb")], writes=[pb])
            OP(dve, lambda e: e.tensor_scalar(out=WkgT[:, 8 * hg:8 * hg + 8, :], in0=p[0:64, :, :], scalar1=cols[0:64, 38:39], scalar2=None, op0=ALU.mult),
               reads=[pb, COLS], writes=[bb("WkgT")])

        adads2 = [fw.new_dsem("adads0"), fw.new_dsem("adads1")]
        badads = fw.new_dsem("badads")
        adarot = {"n": 0}
        adaF = [R1[:, 8 * k_:8 * k_ + 8, :].rearrange("p a b -> p (a b)").bitcast(F32).rearrange("p (k c) -> p k c", k=8) for k_ in range(2)]
        adaH = [R1[:, 16 + 4 * k_:20 + 4 * k_, :].rearrange("p a b -> p (a b)").rearrange("p (k c) -> p k c", k=8) for k_ in range(2)]

        def mod_unit(u):
            col0 = 256 * u
            k_ = adarot["n"] % 2
            adarot["n"] += 1
            wf, wh = adaF[k_], adaH[k_]
            fB = RB(8 * k_, 8 * k_ + 8)
            hB = RB(16 + 4 * k_, 20 + 4 * k_)
            fw.dma(sp, adads2[k_], wf, kview(wada_d, 0, 8, col0, col0 + 256), writes=fB)
            fw.dma(sp, badads, badar[:], bada_d[:, col0:col0 + 256].partition_broadcast(2), writes=[bb("badar")])
            OP(dve, lambda e: e.tensor_copy(out=wh, in_=wf), reads=fB, writes=hB)
            p, pb = pg()
            for kc in range(8):
                OP(pe, lambda e: e.matmul(p[0:2, 0:256], lhsT=cactb[:, kc, :], rhs=wh[:, kc, :], start=(kc == 0), stop=(kc == 7)),
                   reads=[bb("cactb")] + hB, writes=[pb])
            OP(dve, lambda e: e.tensor_tensor(out=rowt[:], in0=p[0:2, 0:256], in1=badar[:, :], op=ALU.add),
               reads=[pb, bb("badar")], writes=[bb("rowt")])

        for vi, v in enumerate([0, 1, 3, 4]):
            for q4 in range(4):
                mod_unit(4 * v + q4)
                p, pb = pg()
                for pc in range(2):
                    OP(pe, lambda e: e.transpose(out=p[:, 2 * pc:2 * pc + 2], in_=rowt[0:2, pc * 128:(pc + 1) * 128], identity=identf[0:2, 0:2]),
                       reads=[bb("rowt"), CST], writes=[pb])
                OP(dve, lambda e: e.tensor_copy(out=modcol[:, vi, 2 * q4:2 * q4 + 2, :], in_=p[:, 0:4].rearrange("p (a b) -> p a b", a=2)),
                   reads=[pb], writes=[bb("modcol")])
        for n in range(2):
            for b in range(2):
                OP(dve, lambda e: e.scalar_tensor_tensor(out=gcol[:, n, :, b], in0=modcol[:, 2 * n + 1, :, b], scalar=1.0,
                                                         in1=cols[:, 8 * n:8 * n + 8], op0=ALU.add, op1=ALU.mult),
                   reads=[bb("modcol"), COLS], writes=[bb("gcol")])

        def compute_gates(b):
            for gi, v in enumerate([2, 5]):
                for q4 in range(4):
                    mod_unit(4 * v + q4)
                    p, pb = pg()
                    OP(pe, lambda e: e.matmul(p[:, 0:256], lhsT=cst[0:2, 483 + 128 * b:483 + 128 * b + 128], rhs=rowt[0:2, :], start=True, stop=True),
                       reads=[bb("rowt"), CST], writes=[pb])
                    OP(act, lambda e: e.activation(out=gate_bc[:, gi, 256 * q4:256 * q4 + 256], in_=p[:, 0:256], func=AF.Copy),
                       reads=[pb], writes=[bb("gate_bc")])

        def rstd_small(src_ap, dst_ap, n_div, nm, width):
            OP(dve, lambda e: e.tensor_scalar(out=dst_ap, in0=src_ap, scalar1=1.0 / n_div, scalar2=EPS, op0=ALU.mult, op1=ALU.add),
               reads=[bb(nm)], writes=[bb(nm)])
            OP(act, lambda e: e.activation(out=dst_ap, in_=dst_ap, func=AF.Sqrt), reads=[bb(nm)], writes=[bb(nm)])
            OP(dve, lambda e: e.reciprocal(out=dst_ap, in_=dst_ap), reads=[bb(nm)], writes=[bb(nm)])

        def bcast_last(ap2, n):
            return bass.AP(ap2.tensor, ap2.offset, [list(ap2.ap[0]), list(ap2.ap[1]), [0, n]])

        def norm_to_hT(b, n):
            OP(pool, lambda e: e.memset(st4[:, 0:4], 0.0), writes=[bb("st4")])
            for tt in range(NTT):
                OP(act, lambda e: e.activation(out=og[:], in_=xt[:, tt, :], func=AF.Square, accum_out=st4[:, tt:tt + 1]),
                   reads=[bb(f"xt{tt}")], writes=[bb("og"), bb("st4")])
            rstd_small(st4[:, 0:4], st4[:, 12:16], float(D), "st4", 4)
            gB = bcast_last(gcol[:, n, :, b], 128)
            sB = bcast_last(modcol[:, 2 * n, :, b], 128)
            for tt in range(NTT):
                OP(act, lambda e: e.activation(out=xn[:], in_=xt[:, tt, :], func=AF.Copy, scale=st4[:, 12 + tt:13 + tt]),
                   reads=[bb(f"xt{tt}"), bb("st4")], writes=[bb("xn")])
                p, pb = ptr()
                for kc in range(8):
                    OP(pe, lambda e: e.transpose(out=p[:, kc, :], in_=xn[:, kc * 128:(kc + 1) * 128], identity=identb[:]),
                       reads=[bb("xn"), bb("identb")], writes=[pb])
                t, tb_ = tf()
                tv = t[:, :].rearrange("p (a c) -> p a c", a=4)
                for half in range(2):
                    ks = slice(4 * half, 4 * half + 4)
                    OP(dve, lambda e: e.tensor_tensor(out=tv, in0=p[:, ks, :], in1=bass.AP(gB.tensor, gB.offset + 4 * half * gB.ap[1][0], [list(gB.ap[0]), [gB.ap[1][0], 4], [0, 128]]), op=ALU.mult),
                       reads=[pb, bb("gcol")], writes=[tb_])
                    OP(dve, lambda e: e.tensor_tensor(out=hT[:, ks, tt * 128:(tt + 1) * 128], in0=tv, in1=bass.AP(sB.tensor, sB.offset + 4 * half * sB.ap[1][0], [list(sB.ap[0]), [sB.ap[1][0], 4], [0, 128]]), op=ALU.add),
                       reads=[tb_, bb("modcol")], writes=[bb("hT")])

        def proj_fm(wv, wbuf, c0, m, rhs_fn, rbufs, nk=8):
            p, pb = pg()
            for kc in range(nk):
                OP(pe, lambda e: e.matmul(p[0:m, 0:TB], lhsT=wv[:, kc, c0:c0 + m], rhs=rhs_fn(kc), start=(kc == 0), stop=(kc == nk - 1)),
                   reads=[wbuf] + rbufs, writes=[pb])
            return p, pb

        def proj_tm(wv, wbuf, tt, ncols):
            p, pb = pg()
            for kc in range(8):
                OP(pe, lambda e: e.matmul(p[:, 0:ncols], lhsT=hT[:, kc, tt * 128:(tt + 1) * 128], rhs=wv[:, kc, 0:ncols], start=(kc == 0), stop=(kc == 7)),
                   reads=[wbuf, bb("hT")], writes=[pb])
            return p, pb

        def rsqrt_big(dst, dstb, src_psum, srcb, ndiv, P=128):
            OP(act, lambda e: e.activation(out=dst[0:P, :], in_=src_psum, func=AF.Ln, scale=1.0 / ndiv, bias=epsc[0:P, :]), reads=[srcb, COLS], writes=[dstb])
            OP(act, lambda e: e.activation(out=dst[0:P, :], in_=dst[0:P, :], func=AF.Exp, scale=-0.5), reads=[dstb], writes=[dstb])

        def sin_of(src, dst, nm_src, nm_dst, shift):
            r = slice(64, 96)
            rr, rrb = tf()
            ta, tab = tf()
            kf, kfb = tf()
            kif, kib = tf()
            kk_i = kif[:, :].bitcast(I32)
            OP(dve, lambda e: e.tensor_scalar(out=rr[r], in0=src[r], scalar1=float(shift - np.pi), scalar2=None, op0=ALU.add),
               reads=[bb(nm_src)], writes=[rrb])
            OP(dve, lambda e: e.tensor_scalar(out=ta[r], in0=rr[r], scalar1=1.0 / TWO_PI, scalar2=None, op0=ALU.mult),
               reads=[rrb], writes=[tab])
            OP(dve, lambda e: e.tensor_copy(out=kk_i[r], in_=ta[r]), reads=[tab], writes=[kib])
            OP(dve, lambda e: e.tensor_copy(out=kf[r], in_=kk_i[r]), reads=[kib], writes=[kfb])
            OP(dve, lambda e: e.scalar_tensor_tensor(out=rr[r], in0=kf[r], scalar=-6.28125, in1=rr[r], op0=ALU.mult, op1=ALU.add),
               reads=[kfb, rrb], writes=[rrb])
            OP(dve, lambda e: e.scalar_tensor_tensor(out=rr[r], in0=kf[r], scalar=-(TWO_PI - 6.28125), in1=rr[r], op0=ALU.mult, op1=ALU.add),
               reads=[kfb, rrb], writes=[rrb])
            OP(dve, lambda e: e.tensor_scalar(out=ta[r], in0=rr[r], scalar1=float(np.pi), scalar2=-TWO_PI, op0=ALU.is_gt, op1=ALU.mult),
               reads=[rrb], writes=[tab])
            OP(pool, lambda e: e.tensor_tensor(out=rr[r], in0=rr[r], in1=ta[r], op=ALU.add), reads=[rrb, tab], writes=[rrb])
            OP(dve, lambda e: e.tensor_scalar(out=ta[r], in0=rr[r], scalar1=-float(np.pi), scalar2=TWO_PI, op0=ALU.is_lt, op1=ALU.mult),
               reads=[rrb], writes=[tab])
            OP(pool, lambda e: e.tensor_tensor(out=rr[r], in0=rr[r], in1=ta[r], op=ALU.add), reads=[rrb, tab], writes=[rrb])
            OP(act, lambda e: e.activation(out=dst[r], in_=rr[r], func=AF.Sin, scale=-1.0), reads=[rrb], writes=[bb(nm_dst)])

        xld = [fw.new_dsem(f"xld{t}") for t in range(NTT)]
        xst = [fw.new_dsem(f"xst{t}") for t in range(NTT)]
        pds = fw.new_dsem("posd")
        dbgs = fw.new_dsem("dbg") if debug else None
        OUTB = Buf("outb")

        def do_block(b, j):
            t0 = j * TB
            NK = (t0 + TB) // 128
            dbg_here = debug and j == 0 and b == 0
            for tt in range(NTT):
                fw.dma(sp, xld[tt], xt[:, tt, :], x_d[b, t0 + tt * 128:t0 + (tt + 1) * 128, :], writes=[bb(f"xt{tt}")])
            fw.dma(sp, pds, posi[:], pos_d[b:b + 1, t0:t0 + TB].partition_broadcast(96), writes=[bb("posi")])
            norm_to_hT(b, 0)
            if dbg_here:
                dump("d_hT", hT[:].rearrange("p a b -> p (a b)"), [bb("hT")], 128, 8 * TB, BF16)

            wv, wbuf = ws_get()
            p, pb = proj_fm(wv, wbuf, 0, 16, lambda kc: hT[:, kc, :], [bb("hT")])
            OP(act, lambda e: e.activation(out=alrT[:], in_=p[0:16, 0:TB], func=AF.Copy), reads=[pb], writes=[bb("alrT")])
            for i in range(2):
                p, pb = proj_fm(wv, wbuf, 16 + 128 * i, 128, lambda kc: hT[:, kc, :], [bb("hT")])
                OP(act, lambda e: e.activation(out=latf[:, i, :], in_=p[:, 0:TB], func=AF.Copy), reads=[pb], writes=[bb(f"latf{i}")])
            wv, wbuf = ws_get()
            p, pb = proj_fm(wv, wbuf, 0, 128, lambda kc: hT[:, kc, :], [bb("hT")])
            OP(act, lambda e: e.activation(out=latf[:, 2, :], in_=p[:, 0:TB], func=AF.Copy), reads=[pb], writes=[bb("latf2")])
            p, pb = proj_fm(wv, wbuf, 64, 96, lambda kc: hT[:, kc, :], [bb("hT")])
            OP(act, lambda e: e.activation(out=kpf[64:96, :], in_=p[64:96, 0:TB], func=AF.Copy, scale=cols[64:96, 38:39]),
               reads=[pb, COLS], writes=[bb("kpf")])
            OP(act, lambda e: e.activation(out=kpq[64:96, :], in_=p[64:96, 0:TB], func=AF.Square), reads=[pb], writes=[bb("kpq")])
            OP(pool, lambda e: e.memset(qA, 0.0), writes=RB(0, 4))
            OP(pool, lambda e: e.memset(qB, 0.0), writes=RB(4, 8))
            wk = [ws_get(), ws_get()]
            for tt in range(NTT):
                p, pb = pg()
                OP(pe, lambda e: e.matmul(p[:, :], lhsT=alrT[0:16, tt * 128:(tt + 1) * 128], rhs=walb[0:16, :], start=True, stop=True),
                   reads=[bb("alrT"), bb("walb")], writes=[pb])
                OP(dve, lambda e: e.tensor_tensor(out=spl[:, :], in0=p[:, :], in1=balb[:], op=ALU.add), reads=[pb, bb("balb")], writes=[bb("spl")])
                OP(act, lambda e: e.activation(out=spl[:, :], in_=spl[:, :], func=AF.Exp, scale=-1.0), reads=[bb("spl")], writes=[bb("spl")])
                OP(act, lambda e: e.activation(out=spl[:, :], in_=spl[:, :], func=AF.Ln, bias=cst[:, 258:259]), reads=[bb("spl"), CST], writes=[bb("spl")])
                p, pb = pg()
                OP(pe, lambda e: e.matmul(p[:, :], lhsT=tri, rhs=spl[:, :], start=True, stop=True), reads=[CST, bb("spl")], writes=[pb])
                OP(act, lambda e: e.activation(out=erev[:, :], in_=p[:, :], func=AF.Exp, scale=-1.0 / 16), reads=[pb], writes=[bb("erev")])
                for i in range(2):
                    wv, wbuf = wk[i]
                    p, pb = proj_tm(wv, wbuf, tt, 256)
                    OP(dve, lambda e: e.tensor_tensor(out=kdec[:, tt, 256 * i:256 * i + 256], in0=p[:, 0:256], in1=erev[:, 256 * i:256 * i + 256], op=ALU.mult),
                       reads=[pb, bb("erev")], writes=RB(8 + tt, 9 + tt))
                p, pb = pg()
                for h in range(4):
                    OP(pe, lambda e: e.matmul(p[:, 2 * h:2 * h + 2], lhsT=spl[:, h * 128:(h + 1) * 128], rhs=ind, start=True, stop=True),
                       reads=[CST, bb("spl")], writes=[pb])
                OP(act, lambda e: e.activation(out=dec[:, tt, :, :], in_=p[:, 0:8].rearrange("p (a b) -> p a b", a=4), func=AF.Exp, scale=-1.0 / 16),
                   reads=[pb], writes=[bb(f"dec{tt}")])
            for i in range(4):
                wv, wbuf = ws_get()
                for tt in range(NTT):
                    p, pb = proj_tm(wv, wbuf, tt, 256)
                    OP(act, lambda e: e.activation(out=gv[:, tt, 256 * i:256 * i + 256], in_=p[:, 0:256], func=AF.Copy), reads=[pb], writes=RB(12 + 2 * tt, 14 + 2 * tt))
            OP(dve, lambda e: e.tensor_copy(out=ang[64:96, :], in_=posi[64:96, :]), reads=[bb("posi")], writes=[bb("ang")])
            OP(dve, lambda e: e.tensor_scalar(out=ang[64:96, :], in0=ang[64:96, :], scalar1=freqc[64:96, :], scalar2=None, op0=ALU.mult),
               reads=[bb("ang"), CST], writes=[bb("ang")])
            sin_of(ang, sinT, "ang", "sinT", 0.0)
            sin_of(ang, cosT, "ang", "cosT", float(np.pi / 2))
            p, pb = pg()
            OP(pe, lambda e: e.matmul(p[0:96, 0:TB], lhsT=permf[0:96, :], rhs=kpf[0:96, :], start=True, stop=True), reads=[CST, bb("kpf")], writes=[pb])
            ta, tab = tf()
            tb2, tb2b = tf()
            OP(dve, lambda e: e.tensor_tensor(out=ta[64:96, :], in0=p[64:96, 0:TB], in1=sinT[64:96, :], op=ALU.mult), reads=[pb, bb("sinT")], writes=[tab])
            OP(pool, lambda e: e.tensor_tensor(out=tb2[64:96, :], in0=kpf[64:96, :], in1=cosT[64:96, :], op=ALU.mult), reads=[bb("kpf"), bb("cosT")], writes=[tb2b])
            OP(pool, lambda e: e.tensor_tensor(out=KR[64:96, t0:t0 + TB], in0=ta[64:96, :], in1=tb2[64:96, :], op=ALU.add), reads=[tab, tb2b], writes=[bb("KR")])
            for li, (chs, ndiv) in enumerate([((0, 1), 256.0), ((2,), 128.0)]):
                sqs = []
                for i in chs:
                    tq, tqb = tf()
                    OP(pool, lambda e: e.tensor_tensor(out=tq[:, :], in0=latf[:, i, :], in1=latf[:, i, :], op=ALU.mult),
                       reads=[bb(f"latf{i}")], writes=[tqb])
                    sqs.append((tq, tqb))
                p, pb = pg()
                for n_, (tq, tqb) in enumerate(sqs):
                    OP(pe, lambda e: e.matmul(p[:, 0:TB], lhsT=onesf, rhs=tq[:, :], start=(n_ == 0), stop=(n_ == len(sqs) - 1)),
                       reads=[CST, tqb], writes=[pb])
                rl, rlb = tf()
                rsqrt_big(rl, rlb, p[:, 0:TB], pb, ndiv)
                for i in chs:
                    if li == 0:
                        OP(dve, lambda e: e.scalar_tensor_tensor(out=latn[:, i, :], in0=latf[:, i, :], scalar=cols[:, 34 + i:35 + i], in1=rl[:, :], op0=ALU.mult, op1=ALU.mult),
                           reads=[bb(f"latf{i}"), COLS, rlb], writes=[bb(f"latn{i}")])
                    else:
                        OP(dve, lambda e: e.scalar_tensor_tensor(out=KC[:, t0:t0 + TB], in0=latf[:, i, :], scalar=cols[:, 36:37], in1=rl[:, :], op0=ALU.mult, op1=ALU.mult),
                           reads=[bb(f"latf{i}"), COLS, rlb], writes=[bb("KC")])

            for i in range(2):
                wv, wbuf = ws_get()
                for hh in range(2):
                    h = 2 * i + hh
                    p, pb = proj_fm(wv, wbuf, 128 * hh, 128, lambda kc: hT[:, kc, :], [bb("hT")])
                    pv = p[:, 0:TB].rearrange("p (c two j) -> p c two j", two=2, j=64)
                    OP(act, lambda e: e.activation(out=qA[:, h, :].rearrange("p (c two j) -> p c two j", two=2, j=64)[:, :, 0, :], in_=pv[:, :, 0, :], func=AF.Copy, scale=128.0 ** -0.5),
                       reads=[pb], writes=RB(h, h + 1))
                    OP(act, lambda e: e.activation(out=qB[:, h, :].rearrange("p (c two j) -> p c two j", two=2, j=64)[:, :, 1, :], in_=pv[:, :, 1, :], func=AF.Copy, scale=128.0 ** -0.5),
                       reads=[pb], writes=RB(4 + h, 5 + h))
            for i in range(4):
                wv, wbuf = ws_get()
                for tt in range(NTT):
                    p, pb = proj_tm(wv, wbuf, tt, 256)
                    OP(act, lambda e: e.activation(out=sg[:, tt, 256 * i:256 * i + 256], in_=p[:, 0:256], func=AF.Silu), reads=[pb], writes=RB(20 + 2 * tt, 22 + 2 * tt))
            for tt in range(NTT):
                kdB = RB(8 + tt, 9 + tt)
                gvB = RB(12 + 2 * tt, 14 + 2 * tt)
                sgB = RB(20 + 2 * tt, 22 + 2 * tt)
                for cc in range(2):
                    rs = slice(64 * cc, 64 * cc + 64)
                    for h in range(4):
                        p, pb = pg()
                        OP(pe, lambda e: e.matmul(p[:, 0:256], lhsT=kdec[rs, tt, h * 128:(h + 1) * 128], rhs=gv[rs, tt, h * 256:(h + 1) * 256], start=True, stop=True),
                           reads=kdB + gvB, writes=[pb])
                        OP(dve, lambda e: e.scalar_tensor_tensor(out=S[:, h, :], in0=S[:, h, :], scalar=dec[:, tt, h, cc:cc + 1], in1=p[:, 0:256], op0=ALU.mult, op1=ALU.add),
                           reads=[bb(f"S{h}"), bb(f"dec{tt}"), pb], writes=[bb(f"S{h}")])
                        OP(act, lambda e: e.activation(out=Sbf[:, cc, h, :], in_=S[:, h, :], func=AF.Copy), reads=[bb(f"S{h}")], writes=[bb(f"Sbf{cc}_{h}")])
                ops_ = []
                for hp in range(2):
                    p, pb = pa()
                    ops_.append((p, pb))
                    for hh in range(2):
                        h = 2 * hp + hh
                        OP(pe, lambda e: e.matmul(p[:, 256 * hh:256 * hh + 256], lhsT=qA[:, h, tt * 128:(tt + 1) * 128], rhs=Sbf[:, 0, h, :], start=True, stop=False),
                           reads=RB(h, h + 1) + [bb(f"Sbf0_{h}")], writes=[pb])
                        OP(pe, lambda e: e.matmul(p[:, 256 * hh:256 * hh + 256], lhsT=qB[:, h, tt * 128:(tt + 1) * 128], rhs=Sbf[:, 1, h, :], start=False, stop=True),
                           reads=RB(4 + h, 5 + h) + [bb(f"Sbf1_{h}")], writes=[pb])
                OP(pool, lambda e: e.memset(st4[:, 4:8], 0.0), writes=[bb("st4")])
                for h in range(4):
                    p, pb = ops_[h // 2]
                    OP(act, lambda e: e.activation(out=junk[:, 0:256], in_=p[:, 256 * (h % 2):256 * (h % 2) + 256], func=AF.Square, accum_out=st4[:, 4 + h:5 + h]),
                       reads=[pb], writes=[bb("junk"), bb("st4")])
                rstd_small(st4[:, 4:8], st4[:, 8:12], 256.0, "st4", 4)
                for h in range(4):
                    p, pb = ops_[h // 2]
                    OP(dve, lambda e: e.scalar_tensor_tensor(out=og[:, 256 * h:256 * h + 256], in0=p[:, 256 * (h % 2):256 * (h % 2) + 256], scalar=st4[:, 8 + h:9 + h],
                                                             in1=sg[:, tt, 256 * h:256 * h + 256], op0=ALU.mult, op1=ALU.mult),
                       reads=[pb, bb("st4")] + sgB, writes=[bb("og")])
                p, pb = ptr()
                for kc in range(8):
                    OP(pe, lambda e: e.transpose(out=p[:, kc, :], in_=og[:, kc * 128:(kc + 1) * 128], identity=identb[:]),
                       reads=[bb("og"), bb("identb")], writes=[pb])
                for par in range(2):
                    pvw = p[:, :, :].rearrange("p (a two) c -> p a two c", two=2)[:, :, par, :]
                    ovw = ogT[:, :, tt * 128:(tt + 1) * 128].rearrange("p (a two) c -> p a two c", two=2)[:, :, par, :]
                    OP(dve, lambda e: e.tensor_scalar(out=ovw, in0=pvw, scalar1=cols[:, 32 + par:33 + par], scalar2=None, op0=ALU.mult),
                       reads=[pb, COLS], writes=[bb("ogT")])
            if dbg_here:
                dump("d_ogT", ogT[:].rearrange("p a b -> p (a b)"), [bb("ogT")], 128, 8 * TB, BF16)

            wq = [ws_get(), ws_get()]
            qst = {}

            def q_s12(h):
                wv, wbuf = wq[h // 8]
                hh = h % 8
                p, pb = PG[h % 2], PGb[h % 2]
                for kc in range(2):
                    OP(pe, lambda e: e.matmul(p[0:96, 0:TB], lhsT=wv[:, kc, 96 * hh:96 * hh + 96], rhs=latn[:, kc, :], start=(kc == 0), stop=(kc == 1)),
                       reads=[wbuf, bb("latn0"), bb("latn1")], writes=[pb])
                t1f, t1b = tf()
                t1 = t1f[:, :].bitcast(BF16)[:, 0:TB]
                OP(act, lambda e: e.activation(out=t1[0:96, :], in_=p[0:96, 0:TB], func=AF.Square), reads=[pb], writes=[t1b])
                qst[h] = (p, pb, t1, t1b)

            def q_s345(h):
                p, pb, t1, t1b = qst[h]
                p2, p2b = PG[2], PGb[2]
                OP(pe, lambda e: e.matmul(p2[0:96, 0:TB], lhsT=onesb[0:96, 0:96], rhs=t1[0:96, :], start=True, stop=True), reads=[bb("onesb"), t1b], writes=[p2b])
                t2, t2b = tf()
                rsqrt_big(t2, t2b, p2[0:96, 0:TB], p2b, 96.0, P=96)
                OP(dve, lambda e: e.scalar_tensor_tensor(out=R1[0:96, h, :], in0=p[0:96, 0:TB], scalar=cols[0:96, 39:40], in1=t2[0:96, :], op0=ALU.mult, op1=ALU.mult),
                   reads=[pb, COLS, t2b], writes=[bb(f"R1_{h}")])

            def q_s67(h):
                p3, p3b = PG[3], PGb[3]
                OP(pe, lambda e: e.matmul(p3[0:96, 0:TB], lhsT=permb[0:96, :], rhs=R1[0:96, h, :], start=True, stop=True), reads=[bb("permb"), bb(f"R1_{h}")], writes=[p3b])
                t3, t3b = tf()
                t4, t4b = tf()
                OP(dve, lambda e: e.tensor_tensor(out=t3[64:96, :], in0=p3[64:96, 0:TB], in1=sinT[64:96, :], op=ALU.mult), reads=[p3b, bb("sinT")], writes=[t3b])
                OP(pool, lambda e: e.tensor_tensor(out=t4[64:96, :], in0=R1[64:96, h, :], in1=cosT[64:96, :], op=ALU.mult), reads=[bb(f"R1_{h}"), bb("cosT")], writes=[t4b])
                OP(dve, lambda e: e.tensor_tensor(out=R1[64:96, h, :], in0=t3[64:96, :], in1=t4[64:96, :], op=ALU.add), reads=[t3b, t4b], writes=[bb(f"R1_{h}")])

            kv_units = []
            kvrot = {"n": 0}

            def kv_bank():
                k_ = kvrot["n"] % 2
                kvrot["n"] += 1
                return PG[4 + k_], PGb[4 + k_]

            def mk_kv0(tt):
                def f():
                    p, pb = kv_bank()
                    OP(pe, lambda e: e.matmul(p[:, 0:1], lhsT=kpq[64:96, tt * 128:(tt + 1) * 128], rhs=onesf[64:96, 0:1], start=True, stop=True),
                       reads=[bb("kpq"), CST], writes=[pb])
                    OP(act, lambda e: e.activation(out=ssk[:, 16:17], in_=p[:, 0:1], func=AF.Copy), reads=[pb], writes=[bb("ssk")])
                return f

            def mk_kvn(tt, n):
                def f():
                    kt = (t0 // 128) + tt
                    p, pb = kv_bank()
                    OP(pe, lambda e: e.matmul(p[:, :], lhsT=KC[:, t0 + tt * 128:t0 + (tt + 1) * 128], rhs=wukv_sb[:, 512 * n:512 * n + 512], start=True, stop=True),
                       reads=[bb("KC"), bb("wukv_sb")], writes=[pb])
                    pv = p[:, :].rearrange("p (h d) -> p h d", h=4)
                    OP(act, lambda e: e.activation(out=VA[:, kt, 4 * n:4 * n + 4, 0:64], in_=pv[:, :, 64:128], func=AF.Copy), reads=[pb], writes=[bb("VA")])
                    OP(act, lambda e: e.activation(out=sqk[:, :, :], in_=pv[:, :, 0:64], func=AF.Square), reads=[pb], writes=[bb("sqk")])
                    OP(dve, lambda e: e.tensor_reduce(out=ssk[:, 4 * n:4 * n + 4], in_=sqk[:, :, :], axis=AX.X, op=ALU.add), reads=[bb("sqk")], writes=[bb("ssk")])
                return f

            def mk_kvr(tt):
                def f():
                    kt = (t0 // 128) + tt
                    OP(dve, lambda e: e.tensor_scalar(out=rsk[:, kt, :], in0=ssk[:, 0:16], scalar1=ssk[:, 16:17], scalar2=1.0 / 96, op0=ALU.add, op1=ALU.mult),
                       reads=[bb("ssk")], writes=[bb("rsk")])
                    OP(dve, lambda e: e.tensor_scalar(out=rsk[:, kt, :], in0=rsk[:, kt, :], scalar1=EPS, scalar2=None, op0=ALU.add), reads=[bb("rsk")], writes=[bb("rsk")])
                    OP(act, lambda e: e.activation(out=rsk[:, kt, :], in_=rsk[:, kt, :], func=AF.Sqrt), reads=[bb("rsk")], writes=[bb("rsk")])
                    OP(dve, lambda e: e.reciprocal(out=rsk[:, kt, :], in_=rsk[:, kt, :]), reads=[bb("rsk")], writes=[bb("rsk")])
                return f

            for tt in range(NTT):
                kv_units.append(mk_kv0(tt))
                for n in range(4):
                    kv_units.append(mk_kvn(tt, n))
                kv_units.append(mk_kvr(tt))

            kvi = 0
            for it in range(-3, 16):
                if 0 <= it + 3 < 16:
                    q_s12(it + 3)
                if 0 <= it + 2 < 16:
                    q_s345(it + 2)
                if 0 <= it < 16:
                    q_s67(it)
                for _ in range(2 if it < 2 else 1):
                    if kvi < len(kv_units):
                        kv_units[kvi]()
                        kvi += 1
            while kvi < len(kv_units):
                kv_units[kvi]()
                kvi += 1
            rot["a"] = 0
            rot["g"] = 0
            if dbg_here:
                dump("d_qT", R1[0:96, 0:16, :].rearrange("p a b -> p (a b)"), RB(0, 16), 96, 16 * TB, BF16)
                dump("d_rsk", rsk[:, 0:NTT, :].rearrange("p a b -> p (a b)"), [bb("rsk")], 128, 16 * NTT, F32)
            srot = {"n": 0}

            def s_bank():
                i = srot["n"] % 4
                srot["n"] += 1
                return PG[i], PGb[i]

            def emit_qa(h):
                p, pb = PTF[0], PTb[0]
                OP(pe, lambda e: e.matmul(p[:, 0:TB], lhsT=WkgT[0:64, h, :], rhs=R1[0:64, h, :], start=True, stop=True), reads=[bb("WkgT"), bb(f"R1_{h}")], writes=[pb])
                qa, qab = qatile()
                OP(dve, lambda e: e.tensor_copy(out=qa[:, :], in_=p[:, 0:TB]), reads=[pb], writes=[qab])
                return qa, qab

            def geom(kt):
                i = kt - j * NTT
                c0 = 128 * i if i > 0 else 0
                return i, c0, TB - c0

            def emit_qk(h, kt, qa, qab):
                i, c0, w = geom(kt)
                p, pb = s_bank()
                OP(pe, lambda e: e.matmul(p[:, 0:w], lhsT=KC[:, kt * 128:(kt + 1) * 128], rhs=qa[:, c0:TB], start=True, stop=False),
                   reads=[bb("KC"), qab], writes=[pb])
                OP(pe, lambda e: e.matmul(p[:, 0:w], lhsT=KR[:, kt * 128:(kt + 1) * 128], rhs=R1[:, h, c0:TB], start=False, stop=True),
                   reads=[bb("KR"), bb(f"R1_{h}")], writes=[pb])
                return p, pb

            def emit_fin(h, oacc, oaccb):
                OP(dve, lambda e: e.tensor_copy(out=drow[64:65, :], in_=oacc[64:65, 0:TB]), reads=[oaccb], writes=[bb("drow")])
                p, pb = PTF[1], PTb[1]
                OP(pe, lambda e: e.matmul(p[0:64, 0:TB], lhsT=onesf[64:65, 0:64], rhs=drow[64:65, :], start=True, stop=True), reads=[CST, bb("drow")], writes=[pb])
                t1, t1b = tf()
                if j == 0:
                    OP(act, lambda e: e.activation(out=t1[0:64, :], in_=p[0:64, 0:TB], func=AF.Ln), reads=[pb], writes=[t1b])
                    OP(act, lambda e: e.activation(out=t1[0:64, :], in_=t1[0:64, :], func=AF.Exp, scale=-1.0), reads=[t1b], writes=[t1b])
                else:
                    OP(dve, lambda e: e.reciprocal(out=t1[0:64, :], in_=p[0:64, 0:TB]), reads=[pb], writes=[t1b])
                OP(dve, lambda e: e.tensor_tensor(out=R1[0:64, 16 + h, :], in0=oacc[0:64, 0:TB], in1=t1[0:64, :], op=ALU.mult), reads=[oaccb, t1b], writes=[bb(f"R1_{16 + h}")])

            steps = [(h, kt) for h in range(16) for kt in range(NK)]
            LA = 3
            qa_of = {0: emit_qa(0)}
            oacc_of = {}
            sq = {}

            def need_qa(hx):
                if hx < 16 and hx not in qa_of:
                    qa_of[hx] = emit_qa(hx)

            for k0 in range(min(LA, len(steps))):
                h2, kt2 = steps[k0]
                need_qa(h2)
                sq[k0] = emit_qk(h2, kt2, *qa_of[h2])
            pending_fin = None
            for si, (h, kt) in enumerate(steps):
                if kt == 0:
                    oacc_of[h] = pa()
                    need_qa(h + 1)
                p, pb = sq.pop(si)
                if si + LA < len(steps):
                    h2, kt2 = steps[si + LA]
                    need_qa(h2)
                    sq[si + LA] = emit_qk(h2, kt2, *qa_of[h2])
                i, c0, w = geom(kt)
                oacc, oaccb = oacc_of[h]
                pt_, ptb_ = ptile()
                OP(act, lambda e: e.activation(out=pt_[:, 0:w], in_=p[:, 0:w], func=AF.Exp, scale=rsk[:, kt, h:h + 1]), reads=[pb, bb("rsk")], writes=[ptb_])
                if i >= 0:
                    OP(act, lambda e: e.memzero(pt_[64:128, 0:64]), writes=[ptb_])
                va = VA[:, kt, h, 0:64]
                va128 = bass.AP(va.tensor, va.offset, [list(va.ap[0]), [64, 2], [1, 64]])
                OP(pe, lambda e: e.matmul(oacc[:, c0:TB], lhsT=va128, rhs=pt_[:, 0:w], start=(kt == 0), stop=(kt == NK - 1)),
                   reads=[bb("VA"), ptb_], writes=[oaccb])
                if pending_fin is not None and (kt >= 1 or NK == 1 or si == len(steps) - 1):
                    emit_fin(*pending_fin)
                    pending_fin = None
                if kt == NK - 1:
                    if pending_fin is not None:
                        emit_fin(*pending_fin)
                    pending_fin = (h, oacc, oaccb)
            if pending_fin is not None:
                emit_fin(*pending_fin)
            if dbg_here:
                dump("d_OT", R1[0:64, 16:32, :].rearrange("p a b -> p (a b)"), RB(16, 32), 64, 16 * TB, BF16)

            for cp in range(4):
                gts = {}
                for gi in range(2):
                    wv, wbuf = ws_get()
                    for i in range(2):
                        c = 2 * cp + i
                        p, pb = proj_fm(wv, wbuf, 128 * i, 128, lambda kc: hT[:, kc, :], [bb("hT")])
                        t, tb_ = tf()
                        OP(act, lambda e: e.activation(out=t[:, :], in_=p[:, 0:TB], func=AF.Sigmoid, bias=cols[:, 16 + 8 * gi + c:17 + 8 * gi + c]),
                           reads=[pb, COLS], writes=[tb_])
                        gts[(gi, i)] = (t, tb_)
                wv, wbuf = ws_get()
                for i in range(2):
                    p, pb = proj_fm(wv, wbuf, 128 * i, 128, lambda kc: ogT[:, kc, :], [bb("ogT")])
                    t, tb_ = gts[(0, i)]
                    OP(dve, lambda e: e.tensor_tensor(out=t[:, :], in0=p[:, 0:TB], in1=t[:, :], op=ALU.mult), reads=[pb, tb_], writes=[tb_])
                for i in range(2):
                    c = 2 * cp + i
                    wv, wbuf = ws_get()
                    p, pb = pg()
                    for h in range(16):
                        OP(pe, lambda e: e.matmul(p[:, 0:TB], lhsT=wv[0:64, h, :], rhs=R1[0:64, 16 + h, :], start=(h == 0), stop=(h == 15)),
                           reads=[wbuf, bb(f"R1_{16 + h}")], writes=[pb])
                    t, tb_ = gts[(1, i)]
                    OP(dve, lambda e: e.tensor_tensor(out=t[:, :], in0=p[:, 0:TB], in1=t[:, :], op=ALU.mult), reads=[pb, tb_], writes=[tb_])
                    ta, tab = gts[(0, i)]
                    OP(pool, lambda e: e.tensor_tensor(out=R1[:, c, :], in0=ta[:, :], in1=t[:, :], op=ALU.add), reads=[tab, tb_], writes=[bb(f"R1_{c}")])
            if dbg_here:
                dump("d_mixT", R1[:, 0:8, :].rearrange("p a b -> p (a b)"), RB(0, 8), 128, 8 * TB, BF16)

            for n in range(4):
                wv, wbuf = ws_get()
                for tt in range(NTT):
                    p, pb = pg()
                    for kc in range(8):
                        OP(pe, lambda e: e.matmul(p[:, 0:256], lhsT=R1[:, kc, tt * 128:(tt + 1) * 128], rhs=wv[:, kc, :], start=(kc == 0), stop=(kc == 7)),
                           reads=[wbuf, bb(f"R1_{kc}")], writes=[pb])
                    t, tb_ = tf()
                    c0 = 256 * n
                    OP(dve, lambda e: e.tensor_tensor(out=t[:, 0:256], in0=p[:, 0:256], in1=gate_bc[:, 0, c0:c0 + 256], op=ALU.mult), reads=[pb, bb("gate_bc")], writes=[tb_])
                    OP(pool, lambda e: e.tensor_tensor(out=xt[:, tt, c0:c0 + 256], in0=xt[:, tt, c0:c0 + 256], in1=t[:, 0:256], op=ALU.add), reads=[tb_, bb(f"xt{tt}")], writes=[bb(f"xt{tt}")])
            if dbg_here:
                dump("d_x1", xt[:].rearrange("p a b -> p (a b)"), [bb(f"xt{t}") for t in range(NTT)], 128, NTT * D, F32)

            norm_to_hT(b, 1)
            for n in range(16):
                wv, wbuf = ws_get()
                for i in range(2):
                    f = 2 * n + i
                    p, pb = proj_fm(wv, wbuf, 128 * i, 128, lambda kc: hT[:, kc, :], [bb("hT")])
                    t, tb_ = tf()
                    OP(act, lambda e: e.activation(out=t[:, :], in_=p[:, 0:TB], func=AF.Relu), reads=[pb], writes=[tb_])
                    OP(pool, lambda e: e.tensor_tensor(out=R1[:, f, :], in0=t[:, :], in1=t[:, :], op=ALU.mult), reads=[tb_], writes=[bb(f"R1_{f}")])
            accs = [(PG[k], PGb[k]) for k in range(4)]
            for half in range(2):
                for n in range(8):
                    wv, wbuf = ws_get()
                    for tt in range(NTT):
                        p, pb = accs[tt]
                        for i in range(4):
                            f = 4 * n + i
                            OP(pe, lambda e: e.matmul(p[:, :], lhsT=R1[:, f, tt * 128:(tt + 1) * 128], rhs=wv[:, i, :],
                                                      start=(n == 0 and i == 0), stop=(n == 7 and i == 3)),
                               reads=[wbuf, bb(f"R1_{f}")], writes=[pb])
                for tt in range(NTT):
                    p, pb = accs[tt]
                    t, tb_ = tf()
                    c0 = 512 * half
                    OP(dve, lambda e: e.tensor_tensor(out=t[:, :], in0=p[:, :], in1=gate_bc[:, 1, c0:c0 + 512], op=ALU.mult), reads=[pb, bb("gate_bc")], writes=[tb_])
                    OP(pool, lambda e: e.tensor_tensor(out=xt[:, tt, c0:c0 + 512], in0=xt[:, tt, c0:c0 + 512], in1=t[:, :], op=ALU.add), reads=[tb_, bb(f"xt{tt}")], writes=[bb(f"xt{tt}")])
            for tt in range(NTT):
                fw.dma(sp, xst[tt], out_d[b, t0 + tt * 128:t0 + (tt + 1) * 128, :], xt[:, tt, :], reads=[bb(f"xt{tt}")])
            rot["g"] = 0

        dumpt = {}

        def dump(name, src, bufs, P, n, dt, three=False):
            if dt == F32:
                fw.dma(sp, dbgs, dbg_d[name], src, reads=bufs)
                return
            if "dmp" not in dumpt:
                dumpt["dmp"] = (fw.sbuf("dmp", [128, TB], F32), Buf("dmp"))
            t, tb_ = dumpt["dmp"]
            for a in range(n // TB):
                piece = src[:, a, :] if three else src[:, a * TB:(a + 1) * TB]
                OP(dve, lambda e: e.tensor_copy(out=t[0:P, :], in_=piece), reads=bufs, writes=[tb_])
                fw.dma(sp, dbgs, dbg_d[name][:, a * TB:(a + 1) * TB], t[0:P, :], reads=[tb_])

        nblk = max_blocks if max_blocks else NBLK
        for b in range(nseq):
            compute_gates(b)
            for h in range(4):
                OP(pool, lambda e: e.memset(S[:, h, :], 0.0), writes=[bb(f"S{h}")])
            for j in range(nblk):
                do_block(b, j)
        for ds in xst + ([dbgs] if debug else []):
            if ds[1] > 0:
                sp.e.wait_ge(ds[0], ds[1])
    return nc


_CACHE = {}


def make_in_maps(inputs, ncores=8):
    f = lambda a: np.ascontiguousarray(np.asarray(a))
    cstv = host_consts()
    shared = {
        "w_ada": f(inputs["w_ada"][0]), "b_ada": f(inputs["b_ada"]), "norm1_g": f(inputs["norm1_g"]),
        "w_in": f(inputs["w_in"][0]), "b_merge": f(inputs["b_merge"]), "gla_w_alpha": f(inputs["gla_w_alpha"][0]),
        "gla_b_alpha": f(inputs["gla_b_alpha"]), "gla_out_norm_g": f(inputs["gla_out_norm_g"]),
        "gla_w_o": f(inputs["gla_w_o"][0]), "mla_q_lat_g": f(inputs["mla_q_lat_g"]), "mla_w_uq": f(inputs["mla_w_uq"][0]),
        "mla_kv_lat_g": f(inputs["mla_kv_lat_g"]), "mla_w_ukv": f(inputs["mla_w_ukv"][0]), "mla_qn_g": f(inputs["mla_qn_g"]),
        "mla_kn_g": f(inputs["mla_kn_g"]), "mla_w_o": f(inputs["mla_w_o"][0]), "w_out": f(inputs["w_out"][0]),
        "norm2_g": f(inputs["norm2_g"]), "mlp_w1": f(inputs["mlp_w1"][0]), "mlp_w2": f(inputs["mlp_w2"][0]), "cst": cstv,
    }
    maps = []
    for c in range(ncores):
        m = dict(shared)
        m["x"] = f(inputs["x"][NB * c:NB * (c + 1)])
        m["c"] = f(inputs["c"][NB * c:NB * (c + 1)])
        m["positions"] = f(inputs["positions"][NB * c:NB * (c + 1)]).astype(np.int32)
        maps.append(m)
    return maps


def kernel(**inputs):
    if "nc" not in _CACHE:
        _CACHE["nc"] = build()
    nc = _CACHE["nc"]
    maps = make_in_maps(inputs)
    res = run_bass_kernel_spmd(nc, maps, core_ids=list(range(8)))
    out = np.concatenate([np.asarray(r["out"]) for r in res.results], axis=0)
    return out.astype(np.float32)
```

```python
import numpy as np
from contextlib import ExitStack
import concourse.bass as bass
import concourse.mybir as mybir
from concourse.bass_utils import run_bass_kernel_spmd

F32 = mybir.dt.float32
BF16 = mybir.dt.bfloat16
I32 = mybir.dt.int32
AF = mybir.ActivationFunctionType
ALU = mybir.AluOpType
AX = mybir.AxisListType

D = 1024
SEQ = 2048
NB = 2
TB = 512
NTT = TB // 128
NBLK = SEQ // TB
EPS = 1e-6
INW = 5552
DFF = 4096
SLOT = 2304
NSLOT = 4
PF = 2
TWO_PI = float(2 * np.pi)
NCST = 739


class Buf:
    __slots__ = ("name", "w", "r")

    def __init__(self, name):
        self.name = name
        self.w = {}
        self.r = {}


class Eng:
    def __init__(self, fw, eng, name, is_pe=False):
        self.e = eng
        self.name = name
        self.is_pe = is_pe
        self.sem = fw.new_sem("s_" + name)
        self.cnt = 0
        self.waited = {}


class FW:
    def __init__(self, nc, stack):
        self.nc = nc
        self.stack = stack
        self.pe = Eng(self, nc.tensor, "pe", is_pe=True)
        self.act = Eng(self, nc.scalar, "act")
        self.dve = Eng(self, nc.vector, "dve")
        self.pool = Eng(self, nc.gpsimd, "pool")
        self.sp = Eng(self, nc.sync, "sp")
        self.engs = [self.pe, self.act, self.dve, self.pool, self.sp]

    def new_sem(self, name):
        return self.stack.enter_context(self.nc.semaphore(name))

    def sbuf(self, name, shape, dt):
        return self.stack.enter_context(self.nc.sbuf_tensor("sb_" + name, list(shape), dt))

    def psum(self, name, shape, dt):
        return self.stack.enter_context(self.nc.psum_tensor("ps_" + name, list(shape), dt))

    def _collect(self, E, reads, writes):
        need = {}

        def add(k, sem, c):
            if k not in need or need[k][1] < c:
                need[k] = (sem, c)

        for b in reads:
            for k, (sem, c) in b.w.items():
                add(k, sem, c)
        for b in writes:
            for k, (sem, c) in b.w.items():
                add(k, sem, c)
            for k, (sem, c) in b.r.items():
                add(k, sem, c)
        if E.is_pe:
            need.pop(E.name, None)
        return need

    def _do_waits(self, E, need):
        for k, (sem, c) in need.items():
            if E.waited.get(k, 0) >= c:
                continue
            E.e.wait_ge(sem, c)
            E.waited[k] = c

    def op(self, E, fn, reads=(), writes=()):
        self._do_waits(E, self._collect(E, reads, writes))
        ins = fn(E.e)
        E.cnt += 1
        ins.then_inc(E.sem, 1)
        for b in reads:
            b.r[E.name] = (E.sem, E.cnt)
        for b in writes:
            b.w = {E.name: (E.sem, E.cnt)}
            b.r = {}
        return ins

    def new_dsem(self, name):
        return [self.new_sem(name), 0, "d_" + name]

    def dma(self, Q, ds, out_ap, in_ap, reads=(), writes=(), **kw):
        self._do_waits(Q, self._collect(Q, reads, writes))
        ins = Q.e.dma_start(out=out_ap, in_=in_ap, **kw)
        ds[1] += 16
        ins.then_inc(ds[0], 16)
        for b in reads:
            b.r[ds[2]] = (ds[0], ds[1])
        for b in writes:
            b.w = {ds[2]: (ds[0], ds[1])}
            b.r = {}
        return ins

    def wait_all(self, E, bufs):
        need = {}
        for b in bufs:
            for d in (b.w, b.r):
                for k, (sem, c) in d.items():
                    if k not in need or need[k][1] < c:
                        need[k] = (sem, c)
        self._do_waits(E, need)


def host_consts():
    c = np.zeros((128, NCST), np.float32)
    c[:, 0:128] = np.eye(128, dtype=np.float32)
    s = np.arange(128)[:, None]
    t = np.arange(128)[None, :]
    c[:, 128:256] = ((s // 64 == t // 64) & (s > t)).astype(np.float32)
    c[:, 256:258] = (s // 64 == np.arange(2)[None, :]).astype(np.float32)
    c[:, 258:386] = 1.0
    perm = np.zeros((128, 96), np.float32)
    for m in range(64, 80):
        perm[m + 16, m] = -1.0
    for m in range(80, 96):
        perm[m - 16, m] = 1.0
    c[:, 386:482] = perm
    fr = (np.float32(10000.0) ** (-np.arange(0, 32, 2, dtype=np.float32) / np.float32(32))).astype(np.float32)
    c[64:80, 482] = fr
    c[80:96, 482] = fr
    for b in range(2):
        c[b, 483 + b * 128: 483 + (b + 1) * 128] = 1.0
    return c


def build(max_blocks=None, nseq=NB, debug=False):
    nc = bass.Bass("TRN2", target_bir_lowering=False)

    def din(name, shape, dt=F32):
        return nc.dram_tensor(name, list(shape), dt, kind="ExternalInput").ap()

    x_d = din("x", [NB, SEQ, D])
    c_d = din("c", [NB, D])
    pos_d = din("positions", [NB, SEQ], I32)
    wada_d = din("w_ada", [D, 6 * D])
    bada_d = din("b_ada", [1, 6 * D])
    n1g_d = din("norm1_g", [1, D])
    win_d = din("w_in", [D, INW])
    bmg_d = din("b_merge", [1, 2 * D])
    wal_d = din("gla_w_alpha", [16, 512])
    bal_d = din("gla_b_alpha", [1, 512])
    gng_d = din("gla_out_norm_g", [1, 256])
    wgo_d = din("gla_w_o", [D, D])
    qlg_d = din("mla_q_lat_g", [1, 256])
    wuq_d = din("mla_w_uq", [256, 1536])
    kvg_d = din("mla_kv_lat_g", [1, 128])
    wukv_d = din("mla_w_ukv", [128, 2048])
    qng_d = din("mla_qn_g", [1, 96])
    kng_d = din("mla_kn_g", [1, 96])
    wmo_d = din("mla_w_o", [D, D])
    wout_d = din("w_out", [D, D])
    n2g_d = din("norm2_g", [1, D])
    w1_d = din("mlp_w1", [D, DFF])
    w2_d = din("mlp_w2", [DFF, D])
    cst_d = din("cst", [128, NCST])
    out_d = nc.dram_tensor("out", [NB, SEQ, D], F32, kind="ExternalOutput").ap()
    dbg_d = {}
    if debug:
        for nm, shp in [("d_hT", [128, 8 * TB]), ("d_ogT", [128, 8 * TB]), ("d_OT", [64, 16 * TB]),
                        ("d_mixT", [128, 8 * TB]), ("d_x1", [128, NTT * D]), ("d_qT", [96, 16 * TB]),
                        ("d_rsk", [128, 16 * NTT])]:
            dbg_d[nm] = nc.dram_tensor(nm, shp, F32, kind="ExternalOutput").ap()

    def scr(name, shape):
        return nc.dram_tensor(name, list(shape), BF16, kind="Internal").ap()

    wada_s = scr("wada_s", [D, 6 * D])
    win_s = scr("win_s", [D, INW])
    wgo_s = scr("wgo_s", [D, D])
    wmo_s = scr("wmo_s", [D, D])
    wout_s = scr("wout_s", [D, D])
    w1_s = scr("w1_s", [D, DFF])
    w2_s = scr("w2_s", [DFF, D])
    wuq_s = scr("wuq_s", [256, 1536])
    wukv_s = scr("wukv_s", [128, 2048])

    with ExitStack() as st:
        fw = FW(nc, st)
        pe, act, dve, pool, sp = fw.pe, fw.act, fw.dve, fw.pool, fw.sp
        OP = fw.op

        wb = {}

        cvq = []
        cv_left = {}

        def cv_emit_one():
            name, b, ds, dst, src = cvq.pop(0)
            fw.dma(pool, ds, dst, src, writes=[b])
            cv_left[name] -= 1

        def cv_ensure(name):
            while cv_left.get(name, 0) > 0:
                cv_emit_one()

        def convert(name, src, dst, rows, piece):
            b = Buf("scr_" + name)
            ds = fw.new_dsem("cv_" + name)
            for r0 in range(0, rows, piece):
                cvq.append((name, b, ds, dst[r0:r0 + piece, :], src[r0:r0 + piece, :]))
            cv_left[name] = cv_left.get(name, 0) + len(range(0, rows, piece))
            wb[name] = b

        def convert_cols(name, src, dst, c0, c1, nsplit=2):
            b = Buf("scr_" + name)
            ds = fw.new_dsem("cv_" + name)
            rows = src.shape[0]
            step = rows // nsplit
            for r0 in range(0, rows, step):
                cvq.append((name, b, ds, dst[r0:r0 + step, c0:c1], src[r0:r0 + step, c0:c1]))
            cv_left[name] = cv_left.get(name, 0) + len(range(0, rows, step))
            wb[name] = b

        def do_conversions():
            convert("wukv", wukv_d, wukv_s, 128, 128)
            convert_cols("win_m", win_d, win_s, 3072, 3504)
            convert_cols("win_k", win_d, win_s, 512, 1024)
            convert_cols("win_v", win_d, win_s, 1024, 2048)
            convert_cols("win_q", win_d, win_s, 0, 512)
            convert_cols("win_g", win_d, win_s, 2048, 3072)
            convert("wuq", wuq_d, wuq_s, 256, 128)
            convert_cols("win_ma", win_d, win_s, 3504, 4528)
            convert_cols("win_mb", win_d, win_s, 4528, 5552)
            convert("wgo", wgo_d, wgo_s, D, 256)
            convert("wmo", wmo_d, wmo_s, D, 256)
            convert("wout", wout_d, wout_s, D, 256)
            convert("w1", w1_d, w1_s, D, 128)
            convert("w2", w2_d, w2_s, DFF, 512)

        assert TB == 512
        cst = fw.sbuf("cst", [128, NCST], F32)
        identb = fw.sbuf("identb", [128, 128], BF16)
        permb = fw.sbuf("permb", [128, 96], BF16)
        onesb = fw.sbuf("onesb", [128, 96], BF16)
        cols = fw.sbuf("cols", [128, 64], F32)
        modcol = fw.sbuf("modcol", [128, 4, 8, 2], F32)
        gcol = fw.sbuf("gcol", [128, 2, 8, 2], F32)
        balb = fw.sbuf("balb", [128, 512], F32)
        walb = fw.sbuf("walb", [16, 512], BF16)
        cT = fw.sbuf("cT", [128, 8, 2], F32)
        cact = fw.sbuf("cact", [128, 8, 2], F32)
        gate_bc = fw.sbuf("gate_bc", [128, 2, D], F32)
        rowt = fw.sbuf("rowt", [2, 256], F32)
        badar = fw.sbuf("badar", [2, 256], F32)
        cactb = fw.sbuf("cactb", [128, 8, 2], BF16)
        slots = [fw.sbuf(f"slot{i}", [128, SLOT], BF16) for i in range(NSLOT)]
        xt = fw.sbuf("xt", [128, NTT, D], F32)
        xn = fw.sbuf("xn", [128, D], BF16)
        junk = fw.sbuf("junk", [128, 256], BF16)
        hT = fw.sbuf("hT", [128, 8, TB], BF16)
        st4 = fw.sbuf("st4", [128, 16], F32)
        KC = fw.sbuf("KC", [128, SEQ], BF16)
        KR = fw.sbuf("KR", [128, SEQ], BF16)
        WkgT = fw.sbuf("WkgT", [64, 16, 128], BF16)
        wukv_sb = fw.sbuf("wukv_sb", [128, 2048], BF16)
        VAf = fw.sbuf("VA", [128, (SEQ // 128) * 16 * 65 + 64], BF16)
        VA = VAf[:, 0:(SEQ // 128) * 16 * 65].rearrange("p (k h d) -> p k h d", k=SEQ // 128, h=16)
        rsk = fw.sbuf("rsk", [128, SEQ // 128, 16], F32)
        S = fw.sbuf("S", [128, 4, 256], F32)
        Sbf = fw.sbuf("Sbf", [128, 2, 4, 256], BF16)
        spl = fw.sbuf("spl", [128, 512], F32)
        erev = fw.sbuf("erev", [128, 512], F32)
        dec = fw.sbuf("dec", [128, NTT, 4, 2], F32)
        alrT = fw.sbuf("alrT", [16, TB], BF16)
        latf = fw.sbuf("latf", [128, 3, TB], F32)
        latn = fw.sbuf("latn", [128, 2, TB], BF16)
        kpf = fw.sbuf("kpf", [96, TB], F32)
        kpq = fw.sbuf("kpq", [96, TB], F32)
        posi = fw.sbuf("posi", [96, TB], I32)
        ang = fw.sbuf("ang", [96, TB], F32)
        cosT = fw.sbuf("cosT", [96, TB], F32)
        sinT = fw.sbuf("sinT", [96, TB], F32)
        R1 = fw.sbuf("R1", [128, 32, TB], BF16)
        ogT = fw.sbuf("ogT", [128, 8, TB], BF16)
        og = fw.sbuf("og", [128, D], BF16)
        pts = [fw.sbuf(f"pt{i}", [128, TB], BF16) for i in range(3)]
        qas = [fw.sbuf(f"qa{i}", [128, TB], BF16) for i in range(2)]
        tmpf = [fw.sbuf(f"tmpf{i}", [128, TB], F32) for i in range(5)]
        ssk = fw.sbuf("ssk", [128, 20], F32)
        sqk = fw.sbuf("sqk", [128, 4, 64], F32)
        drow = fw.sbuf("drow", [65, TB], F32)
        qA = R1[:, 0:4, :]
        qB = R1[:, 4:8, :]
        kdec = R1[:, 8:12, :]
        gv = R1[:, 12:20, :].rearrange("p (t two) b -> p t (two b)", two=2)
        sg = R1[:, 20:28, :].rearrange("p (t two) b -> p t (two b)", two=2)
        adaf = R1[:, 28:32, :].rearrange("p a b -> p (a b)")
        adaslot = adaf.rearrange("p (k c) -> p k c", k=8)

        PG = [fw.psum(f"pg{i}", [128, 512], F32) for i in range(6)]
        PGb = [Buf(f"pg{i}") for i in range(6)]
        PT = [fw.psum(f"ptb{i}", [128, 8, 128], BF16) for i in range(2)]
        PTb = [Buf(f"ptb{i}") for i in range(2)]
        PTF = [t[:, :, :].rearrange("p a b -> p (a b)").bitcast(F32) for t in PT]
        rot = {"g": 0, "t": 0, "a": 0, "pt": 0, "tf": 0}

        def pg():
            i = rot["g"] % 4
            rot["g"] += 1
            return PG[i], PGb[i]

        def pa():
            i = 4 + rot["a"] % 2
            rot["a"] += 1
            return PG[i], PGb[i]

        def ptr():
            i = rot["t"] % 2
            rot["t"] += 1
            return PT[i], PTb[i]

        B = {}

        def bb(n):
            if n not in B:
                B[n] = Buf(n)
            return B[n]

        for i in range(3):
            bb(f"pt{i}")
        for i in range(5):
            bb(f"tmpf{i}")

        def ptile():
            i = rot["pt"] % 3
            rot["pt"] += 1
            return pts[i], B[f"pt{i}"]

        def qatile():
            i = rot.setdefault("qa", 0) % 2
            rot["qa"] += 1
            return qas[i], bb(f"qa{i}")

        def RB(lo, hi):
            return [bb(f"R1_{k}") for k in range(lo, hi)]

        def tf():
            i = rot["tf"] % 5
            rot["tf"] += 1
            return tmpf[i], B[f"tmpf{i}"]

        slotb = [Buf(f"slot{i}") for i in range(NSLOT)]
        slotds = [fw.new_dsem(f"slot{i}") for i in range(NSLOT)]
        ws = {"n": 0, "q": []}

        def ws_issue(view, src_buf):
            i = ws["n"] % NSLOT
            ws["n"] += 1
            P, A, C = view.shape
            dst = slots[i][0:P, 0:A * C].rearrange("p (a c) -> p a c", a=A)
            fw.dma(sp, slotds[i], dst, view, reads=[src_buf], writes=[slotb[i]])
            return dst, slotb[i]

        def kview(scr_ap, kc0, kc1, c0, c1, p=128):
            return scr_ap.rearrange("(kc p) n -> p kc n", p=p)[:, kc0:kc1, c0:c1]

        def block_sched():
            L = []
            L.append(("win_m", kview(win_s, 0, 8, 3072, 3344)))
            L.append(("win_m", kview(win_s, 0, 8, 3344, 3504)))
            for i in range(2, 4):
                L.append(("win_k", kview(win_s, 0, 8, 256 * i, 256 * i + 256)))
            for i in range(4, 8):
                L.append(("win_v", kview(win_s, 0, 8, 256 * i, 256 * i + 256)))
            for i in range(0, 2):
                L.append(("win_q", kview(win_s, 0, 8, 256 * i, 256 * i + 256)))
            for i in range(8, 12):
                L.append(("win_g", kview(win_s, 0, 8, 256 * i, 256 * i + 256)))
            L.append(("wuq", kview(wuq_s, 0, 2, 0, 768)))
            L.append(("wuq", kview(wuq_s, 0, 2, 768, 1536)))
            for cp in range(4):
                L.append(("win_ma", kview(win_s, 0, 8, 3504 + 256 * cp, 3504 + 256 * cp + 256)))
                L.append(("win_mb", kview(win_s, 0, 8, 3504 + 1024 + 256 * cp, 3504 + 1024 + 256 * cp + 256)))
                L.append(("wgo", kview(wgo_s, 0, 8, 256 * cp, 256 * cp + 256)))
                for i in range(2):
                    c = 2 * cp + i
                    L.append(("wmo", kview(wmo_s, 0, 16, 128 * c, 128 * c + 128, p=64)))
            for n in range(4):
                L.append(("wout", kview(wout_s, 0, 8, 256 * n, 256 * n + 256)))
            for n in range(16):
                L.append(("w1", kview(w1_s, 0, 8, 256 * n, 256 * n + 256)))
            for half in range(2):
                for n in range(8):
                    L.append(("w2", kview(w2_s, 4 * n, 4 * n + 4, 512 * half, 512 * half + 512)))
            return L

        SCHED = block_sched()
        total_blocks = nseq * (max_blocks if max_blocks else NBLK)
        wsp = {"req": 0, "iss": 0}

        def ws_get():
            lim = total_blocks * len(SCHED)
            while wsp["iss"] < min(wsp["req"] + PF + 1, lim):
                nm, v = SCHED[wsp["iss"] % len(SCHED)]
                cv_ensure(nm)
                ws["q"].append(ws_issue(v, wb[nm]))
                wsp["iss"] += 1
            wsp["req"] += 1
            if cvq and wsp["req"] >= 4:
                cv_emit_one()
            return ws["q"].pop(0)

        cds = fw.new_dsem("cst")

        def small(dst, src, buf):
            fw.dma(sp, cds, dst, src, writes=[buf], allow_slow_non_contiguous=True)

        small(cst[:], cst_d, bb("cst"))
        small(cols[:, 0:8], n1g_d.rearrange("o (c p) -> p (o c)", p=128), bb("cols"))
        small(cols[:, 8:16], n2g_d.rearrange("o (c p) -> p (o c)", p=128), bb("cols"))
        small(cols[:, 16:32], bmg_d.rearrange("o (c p) -> p (o c)", p=128), bb("cols"))
        small(cols[:, 32:34], gng_d.rearrange("o (c p) -> p (o c)", p=128), bb("cols"))
        small(cols[:, 34:36], qlg_d.rearrange("o (c p) -> p (o c)", p=128), bb("cols"))
        small(cols[:, 36:37], kvg_d.rearrange("o (c p) -> p (o c)", p=128), bb("cols"))
        small(cols[0:96, 37:38], qng_d.rearrange("o (c p) -> p (o c)", p=96), bb("cols"))
        small(cols[0:96, 38:39], kng_d.rearrange("o (c p) -> p (o c)", p=96), bb("cols"))
        small(balb[:], bal_d.partition_broadcast(128), bb("balb"))
        for b_ in range(2):
            small(cT[:, :, b_], c_d[b_:b_ + 1, :].rearrange("o (c p) -> p (o c)", p=128), bb("cT"))
        for nm_ in ["cst", "cols", "balb", "cT"]:
            B[nm_].w = {cds[2]: (cds[0], cds[1])}
        wads = fw.new_dsem("wal")
        fw.dma(pool, wads, walb[:], wal_d, writes=[bb("walb")])

        identf = cst[:, 0:128]
        tri = cst[:, 128:256]
        ind = cst[:, 256:258]
        onesf = cst[:, 258:386]
        permf = cst[:, 386:482]
        freqc = cst[:, 482:483]
        CST = bb("cst")
        COLS = bb("cols")

        OP(dve, lambda e: e.tensor_copy(out=identb[:], in_=identf), reads=[CST], writes=[bb("identb")])
        OP(dve, lambda e: e.tensor_copy(out=permb[:], in_=permf), reads=[CST], writes=[bb("permb")])
        OP(dve, lambda e: e.tensor_copy(out=onesb[:], in_=onesf[:, 0:96]), reads=[CST], writes=[bb("onesb")])
        OP(dve, lambda e: e.tensor_scalar(out=cols[0:96, 39:40], in0=cols[0:96, 37:38], scalar1=96.0 ** -0.5, scalar2=None, op0=ALU.mult), reads=[COLS], writes=[COLS])
        OP(act, lambda e: e.activation(out=cact[:], in_=cT[:], func=AF.Silu), reads=[bb("cT")], writes=[bb("cact")])
        OP(dve, lambda e: e.tensor_copy(out=cactb[:], in_=cact[:]), reads=[bb("cact")], writes=[bb("cactb")])
        OP(pool, lambda e: e.memset(VAf[:], 1.0), writes=[bb("VA")])
        OP(pool, lambda e: e.memset(kpf[:], 0.0), writes=[bb("kpf")])
        OP(pool, lambda e: e.memset(KR[:], 0.0), writes=[bb("KR")])
        OP(pool, lambda e: e.memset(R1[:], 0.0), writes=RB(0, 32))
        OP(pool, lambda e: e.memset(cols[:, 40:41], EPS), reads=[COLS], writes=[COLS])
        do_conversions()
        for nm_ in ["wukv", "win_m", "win_k", "win_v", "win_q", "win_g"]:
            cv_ensure(nm_)
        epsc = cols[:, 40:41]
        wkds = fw.new_dsem("wkds")
        fw.dma(sp, wkds, wukv_sb[:], wukv_s, reads=[wb["wukv"]], writes=[bb("wukv_sb")])
        for hg in range(2):
            p, pb = ptr()
            for hh in range(8):
                h = 8 * hg + hh
                OP(pe, lambda e: e.transpose(out=p[0:64, hh, :], in_=wukv_sb[:, 128 * h:128 * h + 64], identity=identb[:]),
                   reads=[bb("wukv_sb"), bb("identb")], writes=[pb])
            OP(dve, lambda e: e.tensor_scalar(out=WkgT[:, 8 * hg:8 * hg + 8, :], in0=p[0:64, :, :], scalar1=cols[0:64, 38:39], scalar2=None, op0=ALU.mult),
               reads=[pb, COLS], writes=[bb("WkgT")])

        adads2 = [fw.new_dsem("adads0"), fw.new_dsem("adads1")]
        badads = fw.new_dsem("badads")
        adarot = {"n": 0}
        adaF = [R1[:, 8 * k_:8 * k_ + 8, :].rearrange("p a b -> p (a b)").bitcast(F32).rearrange("p (k c) -> p k c", k=8) for k_ in range(2)]
        adaH = [R1[:, 16 + 4 * k_:20 + 4 * k_, :].rearrange("p a b -> p (a b)").rearrange("p (k c) -> p k c", k=8) for k_ in range(2)]

        def mod_unit(u):
            col0 = 256 * u
            k_ = adarot["n"] % 2
            adarot["n"] += 1
            wf, wh = adaF[k_], adaH[k_]
            fB = RB(8 * k_, 8 * k_ + 8)
            hB = RB(16 + 4 * k_, 20 + 4 * k_)
            fw.dma(sp, adads2[k_], wf, kview(wada_d, 0, 8, col0, col0 + 256), writes=fB)
            fw.dma(sp, badads, badar[:], bada_d[:, col0:col0 + 256].partition_broadcast(2), writes=[bb("badar")])
            OP(dve, lambda e: e.tensor_copy(out=wh, in_=wf), reads=fB, writes=hB)
            p, pb = pg()
            for kc in range(8):
                OP(pe, lambda e: e.matmul(p[0:2, 0:256], lhsT=cactb[:, kc, :], rhs=wh[:, kc, :], start=(kc == 0), stop=(kc == 7)),
                   reads=[bb("cactb")] + hB, writes=[pb])
            OP(dve, lambda e: e.tensor_tensor(out=rowt[:], in0=p[0:2, 0:256], in1=badar[:, :], op=ALU.add),
               reads=[pb, bb("badar")], writes=[bb("rowt")])

        for vi, v in enumerate([0, 1, 3, 4]):
            for q4 in range(4):
                mod_unit(4 * v + q4)
                p, pb = pg()
                for pc in range(2):
                    OP(pe, lambda e: e.transpose(out=p[:, 2 * pc:2 * pc + 2], in_=rowt[0:2, pc * 128:(pc + 1) * 128], identity=identf[0:2, 0:2]),
                       reads=[bb("rowt"), CST], writes=[pb])
                OP(dve, lambda e: e.tensor_copy(out=modcol[:, vi, 2 * q4:2 * q4 + 2, :], in_=p[:, 0:4].rearrange("p (a b) -> p a b", a=2)),
                   reads=[pb], writes=[bb("modcol")])
        for n in range(2):
            for b in range(2):
                OP(dve, lambda e: e.scalar_tensor_tensor(out=gcol[:, n, :, b], in0=modcol[:, 2 * n + 1, :, b], scalar=1.0,
                                                         in1=cols[:, 8 * n:8 * n + 8], op0=ALU.add, op1=ALU.mult),
                   reads=[bb("modcol"), COLS], writes=[bb("gcol")])

        def compute_gates(b):
            for gi, v in enumerate([2, 5]):
                for q4 in range(4):
                    mod_unit(4 * v + q4)
                    p, pb = pg()
                    OP(pe, lambda e: e.matmul(p[:, 0:256], lhsT=cst[0:2, 483 + 128 * b:483 + 128 * b + 128], rhs=rowt[0:2, :], start=True, stop=True),
                       reads=[bb("rowt"), CST], writes=[pb])
                    OP(act, lambda e: e.activation(out=gate_bc[:, gi, 256 * q4:256 * q4 + 256], in_=p[:, 0:256], func=AF.Copy),
                       reads=[pb], writes=[bb("gate_bc")])

        def rstd_small(src_ap, dst_ap, n_div, nm, width):
            OP(dve, lambda e: e.tensor_scalar(out=dst_ap, in0=src_ap, scalar1=1.0 / n_div, scalar2=EPS, op0=ALU.mult, op1=ALU.add),
               reads=[bb(nm)], writes=[bb(nm)])
            OP(act, lambda e: e.activation(out=dst_ap, in_=dst_ap, func=AF.Sqrt), reads=[bb(nm)], writes=[bb(nm)])
            OP(dve, lambda e: e.reciprocal(out=dst_ap, in_=dst_ap), reads=[bb(nm)], writes=[bb(nm)])

        def bcast_last(ap2, n):
            return bass.AP(ap2.tensor, ap2.offset, [list(ap2.ap[0]), list(ap2.ap[1]), [0, n]])

        def norm_to_hT(b, n):
            OP(pool, lambda e: e.memset(st4[:, 0:4], 0.0), writes=[bb("st4")])
            for tt in range(NTT):
                OP(act, lambda e: e.activation(out=og[:], in_=xt[:, tt, :], func=AF.Square, accum_out=st4[:, tt:tt + 1]),
                   reads=[bb(f"xt{tt}")], writes=[bb("og"), bb("st4")])
            rstd_small(st4[:, 0:4], st4[:, 12:16], float(D), "st4", 4)
            gB = bcast_last(gcol[:, n, :, b], 128)
            sB = bcast_last(modcol[:, 2 * n, :, b], 128)
            for tt in range(NTT):
                OP(act, lambda e: e.activation(out=xn[:], in_=xt[:, tt, :], func=AF.Copy, scale=st4[:, 12 + tt:13 + tt]),
                   reads=[bb(f"xt{tt}"), bb("st4")], writes=[bb("xn")])
                p, pb = ptr()
                for kc in range(8):
                    OP(pe, lambda e: e.transpose(out=p[:, kc, :], in_=xn[:, kc * 128:(kc + 1) * 128], identity=identb[:]),
                       reads=[bb("xn"), bb("identb")], writes=[pb])
                t, tb_ = tf()
                tv = t[:, :].rearrange("p (a c) -> p a c", a=4)
                for half in range(2):
                    ks = slice(4 * half, 4 * half + 4)
                    OP(dve, lambda e: e.tensor_tensor(out=tv, in0=p[:, ks, :], in1=bass.AP(gB.tensor, gB.offset + 4 * half * gB.ap[1][0], [list(gB.ap[0]), [gB.ap[1][0], 4], [0, 128]]), op=ALU.mult),
                       reads=[pb, bb("gcol")], writes=[tb_])
                    OP(dve, lambda e: e.tensor_tensor(out=hT[:, ks, tt * 128:(tt + 1) * 128], in0=tv, in1=bass.AP(sB.tensor, sB.offset + 4 * half * sB.ap[1][0], [list(sB.ap[0]), [sB.ap[1][0], 4], [0, 128]]), op=ALU.add),
                       reads=[tb_, bb("modcol")], writes=[bb("hT")])

        def proj_fm(wv, wbuf, c0, m, rhs_fn, rbufs, nk=8):
            p, pb = pg()
            for kc in range(nk):
                OP(pe, lambda e: e.matmul(p[0:m, 0:TB], lhsT=wv[:, kc, c0:c0 + m], rhs=rhs_fn(kc), start=(kc == 0), stop=(kc == nk - 1)),
                   reads=[wbuf] + rbufs, writes=[pb])
            return p, pb

        def proj_tm(wv, wbuf, tt, ncols):
            p, pb = pg()
            for kc in range(8):
                OP(pe, lambda e: e.matmul(p[:, 0:ncols], lhsT=hT[:, kc, tt * 128:(tt + 1) * 128], rhs=wv[:, kc, 0:ncols], start=(kc == 0), stop=(kc == 7)),
                   reads=[wbuf, bb("hT")], writes=[pb])
            return p, pb

        def rsqrt_big(dst, dstb, src_psum, srcb, ndiv, P=128):
            OP(act, lambda e: e.activation(out=dst[0:P, :], in_=src_psum, func=AF.Ln, scale=1.0 / ndiv, bias=epsc[0:P, :]), reads=[srcb, COLS], writes=[dstb])
            OP(act, lambda e: e.activation(out=dst[0:P, :], in_=dst[0:P, :], func=AF.Exp, scale=-0.5), reads=[dstb], writes=[dstb])

        def sin_of(src, dst, nm_src, nm_dst, shift):
            r = slice(64, 96)
            rr, rrb = tf()
            ta, tab = tf()
            kf, kfb = tf()
            kif, kib = tf()
            kk_i = kif[:, :].bitcast(I32)
            OP(dve, lambda e: e.tensor_scalar(out=rr[r], in0=src[r], scalar1=float(shift - np.pi), scalar2=None, op0=ALU.add),
               reads=[bb(nm_src)], writes=[rrb])
            OP(dve, lambda e: e.tensor_scalar(out=ta[r], in0=rr[r], scalar1=1.0 / TWO_PI, scalar2=None, op0=ALU.mult),
               reads=[rrb], writes=[tab])
            OP(dve, lambda e: e.tensor_copy(out=kk_i[r], in_=ta[r]), reads=[tab], writes=[kib])
            OP(dve, lambda e: e.tensor_copy(out=kf[r], in_=kk_i[r]), reads=[kib], writes=[kfb])
            OP(dve, lambda e: e.scalar_tensor_tensor(out=rr[r], in0=kf[r], scalar=-6.28125, in1=rr[r], op0=ALU.mult, op1=ALU.add),
               reads=[kfb, rrb], writes=[rrb])
            OP(dve, lambda e: e.scalar_tensor_tensor(out=rr[r], in0=kf[r], scalar=-(TWO_PI - 6.28125), in1=rr[r], op0=ALU.mult, op1=ALU.add),
               reads=[kfb, rrb], writes=[rrb])
            OP(dve, lambda e: e.tensor_scalar(out=ta[r], in0=rr[r], scalar1=float(np.pi), scalar2=-TWO_PI, op0=ALU.is_gt, op1=ALU.mult),
               reads=[rrb], writes=[tab])
            OP(pool, lambda e: e.tensor_tensor(out=rr[r], in0=rr[r], in1=ta[r], op=ALU.add), reads=[rrb, tab], writes=[rrb])
            OP(dve, lambda e: e.tensor_scalar(out=ta[r], in0=rr[r], scalar1=-float(np.pi), scalar2=TWO_PI, op0=ALU.is_lt, op1=ALU.mult),
               reads=[rrb], writes=[tab])
            OP(pool, lambda e: e.tensor_tensor(out=rr[r], in0=rr[r], in1=ta[r], op=ALU.add), reads=[rrb, tab], writes=[rrb])
            OP(act, lambda e: e.activation(out=dst[r], in_=rr[r], func=AF.Sin, scale=-1.0), reads=[rrb], writes=[bb(nm_dst)])

        xld = [fw.new_dsem(f"xld{t}") for t in range(NTT)]
        xst = [fw.new_dsem(f"xst{t}") for t in range(NTT)]
        pds = fw.new_dsem("posd")
        dbgs = fw.new_dsem("dbg") if debug else None
        OUTB = Buf("outb")

        def do_block(b, j):
            t0 = j * TB
            NK = (t0 + TB) // 128
            dbg_here = debug and j == 0 and b == 0
            for tt in range(NTT):
                fw.dma(sp, xld[tt], xt[:, tt, :], x_d[b, t0 + tt * 128:t0 + (tt + 1) * 128, :], writes=[bb(f"xt{tt}")])
            fw.dma(sp, pds, posi[:], pos_d[b:b + 1, t0:t0 + TB].partition_broadcast(96), writes=[bb("posi")])
            norm_to_hT(b, 0)
            if dbg_here:
                dump("d_hT", hT[:].rearrange("p a b -> p (a b)"), [bb("hT")], 128, 8 * TB, BF16)

            wv, wbuf = ws_get()
            p, pb = proj_fm(wv, wbuf, 0, 16, lambda kc: hT[:, kc, :], [bb("hT")])
            OP(act, lambda e: e.activation(out=alrT[:], in_=p[0:16, 0:TB], func=AF.Copy), reads=[pb], writes=[bb("alrT")])
            for i in range(2):
                p, pb = proj_fm(wv, wbuf, 16 + 128 * i, 128, lambda kc: hT[:, kc, :], [bb("hT")])
                OP(act, lambda e: e.activation(out=latf[:, i, :], in_=p[:, 0:TB], func=AF.Copy), reads=[pb], writes=[bb(f"latf{i}")])
            wv, wbuf = ws_get()
            p, pb = proj_fm(wv, wbuf, 0, 128, lambda kc: hT[:, kc, :], [bb("hT")])
            OP(act, lambda e: e.activation(out=latf[:, 2, :], in_=p[:, 0:TB], func=AF.Copy), reads=[pb], writes=[bb("latf2")])
            p, pb = proj_fm(wv, wbuf, 64, 96, lambda kc: hT[:, kc, :], [bb("hT")])
            OP(act, lambda e: e.activation(out=kpf[64:96, :], in_=p[64:96, 0:TB], func=AF.Copy, scale=cols[64:96, 38:39]),
               reads=[pb, COLS], writes=[bb("kpf")])
            OP(act, lambda e: e.activation(out=kpq[64:96, :], in_=p[64:96, 0:TB], func=AF.Square), reads=[pb], writes=[bb("kpq")])
            OP(dve, lambda e: e.tensor_copy(out=ang[64:96, :], in_=posi[64:96, :]), reads=[bb("posi")], writes=[bb("ang")])
            OP(dve, lambda e: e.tensor_scalar(out=ang[64:96, :], in0=ang[64:96, :], scalar1=freqc[64:96, :], scalar2=None, op0=ALU.mult),
               reads=[bb("ang"), CST], writes=[bb("ang")])
            sin_of(ang, sinT, "ang", "sinT", 0.0)
            sin_of(ang, cosT, "ang", "cosT", float(np.pi / 2))
            p, pb = pg()
            OP(pe, lambda e: e.matmul(p[0:96, 0:TB], lhsT=permf[0:96, :], rhs=kpf[0:96, :], start=True, stop=True), reads=[CST, bb("kpf")], writes=[pb])
            ta, tab = tf()
            tb2, tb2b = tf()
            OP(dve, lambda e: e.tensor_tensor(out=ta[64:96, :], in0=p[64:96, 0:TB], in1=sinT[64:96, :], op=ALU.mult), reads=[pb, bb("sinT")], writes=[tab])
            OP(pool, lambda e: e.tensor_tensor(out=tb2[64:96, :], in0=kpf[64:96, :], in1=cosT[64:96, :], op=ALU.mult), reads=[bb("kpf"), bb("cosT")], writes=[tb2b])
            OP(pool, lambda e: e.tensor_tensor(out=KR[64:96, t0:t0 + TB], in0=ta[64:96, :], in1=tb2[64:96, :], op=ALU.add), reads=[tab, tb2b], writes=[bb("KR")])
            OP(pool, lambda e: e.memset(qA, 0.0), writes=RB(0, 4))
            OP(pool, lambda e: e.memset(qB, 0.0), writes=RB(4, 8))
            wk = [ws_get(), ws_get()]
            for tt in range(NTT):
                p, pb = pg()
                OP(pe, lambda e: e.matmul(p[:, :], lhsT=alrT[0:16, tt * 128:(tt + 1) * 128], rhs=walb[0:16, :], start=True, stop=True),
                   reads=[bb("alrT"), bb("walb")], writes=[pb])
                OP(dve, lambda e: e.tensor_tensor(out=spl[:, :], in0=p[:, :], in1=balb[:], op=ALU.add), reads=[pb, bb("balb")], writes=[bb("spl")])
                OP(act, lambda e: e.activation(out=spl[:, :], in_=spl[:, :], func=AF.Exp, scale=-1.0), reads=[bb("spl")], writes=[bb("spl")])
                OP(act, lambda e: e.activation(out=spl[:, :], in_=spl[:, :], func=AF.Ln, bias=cst[:, 258:259]), reads=[bb("spl"), CST], writes=[bb("spl")])
                p, pb = pg()
                OP(pe, lambda e: e.matmul(p[:, :], lhsT=tri, rhs=spl[:, :], start=True, stop=True), reads=[CST, bb("spl")], writes=[pb])
                OP(act, lambda e: e.activation(out=erev[:, :], in_=p[:, :], func=AF.Exp, scale=-1.0 / 16), reads=[pb], writes=[bb("erev")])
                for i in range(2):
                    wv, wbuf = wk[i]
                    p, pb = proj_tm(wv, wbuf, tt, 256)
                    OP(dve, lambda e: e.tensor_tensor(out=kdec[:, tt, 256 * i:256 * i + 256], in0=p[:, 0:256], in1=erev[:, 256 * i:256 * i + 256], op=ALU.mult),
                       reads=[pb, bb("erev")], writes=RB(8 + tt, 9 + tt))
                p, pb = pg()
                for h in range(4):
                    OP(pe, lambda e: e.matmul(p[:, 2 * h:2 * h + 2], lhsT=spl[:, h * 128:(h + 1) * 128], rhs=ind, start=True, stop=True),
                       reads=[CST, bb("spl")], writes=[pb])
                OP(act, lambda e: e.activation(out=dec[:, tt, :, :], in_=p[:, 0:8].rearrange("p (a b) -> p a b", a=4), func=AF.Exp, scale=-1.0 / 16),
                   reads=[pb], writes=[bb(f"dec{tt}")])
            for i in range(4):
                wv, wbuf = ws_get()
                for tt in range(NTT):
                    p, pb = proj_tm(wv, wbuf, tt, 256)
                    OP(act, lambda e: e.activation(out=gv[:, tt, 256 * i:256 * i + 256], in_=p[:, 0:256], func=AF.Copy), reads=[pb], writes=RB(12 + 2 * tt, 14 + 2 * tt))
            for li, (chs, ndiv) in enumerate([((0, 1), 256.0), ((2,), 128.0)]):
                sqs = []
                for i in chs:
                    tq, tqb = tf()
                    OP(pool, lambda e: e.tensor_tensor(out=tq[:, :], in0=latf[:, i, :], in1=latf[:, i, :], op=ALU.mult),
                       reads=[bb(f"latf{i}")], writes=[tqb])
                    sqs.append((tq, tqb))
                p, pb = pg()
                for n_, (tq, tqb) in enumerate(sqs):
                    OP(pe, lambda e: e.matmul(p[:, 0:TB], lhsT=onesf, rhs=tq[:, :], start=(n_ == 0), stop=(n_ == len(sqs) - 1)),
                       reads=[CST, tqb], writes=[pb])
                rl, rlb = tf()
                rsqrt_big(rl, rlb, p[:, 0:TB], pb, ndiv)
                for i in chs:
                    if li == 0:
                        OP(dve, lambda e: e.scalar_tensor_tensor(out=latn[:, i, :], in0=latf[:, i, :], scalar=cols[:, 34 + i:35 + i], in1=rl[:, :], op0=ALU.mult, op1=ALU.mult),
                           reads=[bb(f"latf{i}"), COLS, rlb], writes=[bb(f"latn{i}")])
                    else:
                        OP(dve, lambda e: e.scalar_tensor_tensor(out=KC[:, t0:t0 + TB], in0=latf[:, i, :], scalar=cols[:, 36:37], in1=rl[:, :], op0=ALU.mult, op1=ALU.mult),
                           reads=[bb(f"latf{i}"), COLS, rlb], writes=[bb("KC")])

            for i in range(2):
                wv, wbuf = ws_get()
                for hh in range(2):
                    h = 2 * i + hh
                    p, pb = proj_fm(wv, wbuf, 128 * hh, 128, lambda kc: hT[:, kc, :], [bb("hT")])
                    pv = p[:, 0:TB].rearrange("p (c two j) -> p c two j", two=2, j=64)
                    OP(act, lambda e: e.activation(out=qA[:, h, :].rearrange("p (c two j) -> p c two j", two=2, j=64)[:, :, 0, :], in_=pv[:, :, 0, :], func=AF.Copy, scale=128.0 ** -0.5),
                       reads=[pb], writes=RB(h, h + 1))
                    OP(act, lambda e: e.activation(out=qB[:, h, :].rearrange("p (c two j) -> p c two j", two=2, j=64)[:, :, 1, :], in_=pv[:, :, 1, :], func=AF.Copy, scale=128.0 ** -0.5),
                       reads=[pb], writes=RB(4 + h, 5 + h))
            for i in range(4):
                wv, wbuf = ws_get()
                for tt in range(NTT):
                    p, pb = proj_tm(wv, wbuf, tt, 256)
                    OP(act, lambda e: e.activation(out=sg[:, tt, 256 * i:256 * i + 256], in_=p[:, 0:256], func=AF.Silu), reads=[pb], writes=RB(20 + 2 * tt, 22 + 2 * tt))
            for tt in range(NTT):
                kdB = RB(8 + tt, 9 + tt)
                gvB = RB(12 + 2 * tt, 14 + 2 * tt)
                sgB = RB(20 + 2 * tt, 22 + 2 * tt)
                for cc in range(2):
                    rs = slice(64 * cc, 64 * cc + 64)
                    for h in range(4):
                        p, pb = pg()
                        OP(pe, lambda e: e.matmul(p[:, 0:256], lhsT=kdec[rs, tt, h * 128:(h + 1) * 128], rhs=gv[rs, tt, h * 256:(h + 1) * 256], start=True, stop=True),
                           reads=kdB + gvB, writes=[pb])
                        OP(dve, lambda e: e.scalar_tensor_tensor(out=S[:, h, :], in0=S[:, h, :], scalar=dec[:, tt, h, cc:cc + 1], in1=p[:, 0:256], op0=ALU.mult, op1=ALU.add),
                           reads=[bb(f"S{h}"), bb(f"dec{tt}"), pb], writes=[bb(f"S{h}")])
                        OP(act, lambda e: e.activation(out=Sbf[:, cc, h, :], in_=S[:, h, :], func=AF.Copy), reads=[bb(f"S{h}")], writes=[bb(f"Sbf{cc}_{h}")])
                ops_ = []
                for hp in range(2):
                    p, pb = pa()
                    ops_.append((p, pb))
                    for hh in range(2):
                        h = 2 * hp + hh
                        OP(pe, lambda e: e.matmul(p[:, 256 * hh:256 * hh + 256], lhsT=qA[:, h, tt * 128:(tt + 1) * 128], rhs=Sbf[:, 0, h, :], start=True, stop=False),
                           reads=RB(h, h + 1) + [bb(f"Sbf0_{h}")], writes=[pb])
                        OP(pe, lambda e: e.matmul(p[:, 256 * hh:256 * hh + 256], lhsT=qB[:, h, tt * 128:(tt + 1) * 128], rhs=Sbf[:, 1, h, :], start=False, stop=True),
                           reads=RB(4 + h, 5 + h) + [bb(f"Sbf1_{h}")], writes=[pb])
                OP(pool, lambda e: e.memset(st4[:, 4:8], 0.0), writes=[bb("st4")])
                for h in range(4):
                    p, pb = ops_[h // 2]
                    OP(act, lambda e: e.activation(out=junk[:, 0:256], in_=p[:, 256 * (h % 2):256 * (h % 2) + 256], func=AF.Square, accum_out=st4[:, 4 + h:5 + h]),
                       reads=[pb], writes=[bb("junk"), bb("st4")])
                rstd_small(st4[:, 4:8], st4[:, 8:12], 256.0, "st4", 4)
                for h in range(4):
                    p, pb = ops_[h // 2]
                    OP(dve, lambda e: e.scalar_tensor_tensor(out=og[:, 256 * h:256 * h + 256], in0=p[:, 256 * (h % 2):256 * (h % 2) + 256], scalar=st4[:, 8 + h:9 + h],
                                                             in1=sg[:, tt, 256 * h:256 * h + 256], op0=ALU.mult, op1=ALU.mult),
                       reads=[pb, bb("st4")] + sgB, writes=[bb("og")])
                p, pb = ptr()
                for kc in range(8):
                    OP(pe, lambda e: e.transpose(out=p[:, kc, :], in_=og[:, kc * 128:(kc + 1) * 128], identity=identb[:]),
                       reads=[bb("og"), bb("identb")], writes=[pb])
                for par in range(2):
                    pvw = p[:, :, :].rearrange("p (a two) c -> p a two c", two=2)[:, :, par, :]
                    ovw = ogT[:, :, tt * 128:(tt + 1) * 128].rearrange("p (a two) c -> p a two c", two=2)[:, :, par, :]
                    OP(dve, lambda e: e.tensor_scalar(out=ovw, in0=pvw, scalar1=cols[:, 32 + par:33 + par], scalar2=None, op0=ALU.mult),
                       reads=[pb, COLS], writes=[bb("ogT")])
            if dbg_here:
                dump("d_ogT", ogT[:].rearrange("p a b -> p (a b)"), [bb("ogT")], 128, 8 * TB, BF16)

            wq = [ws_get(), ws_get()]
            qst = {}

            def q_s12(h):
                wv, wbuf = wq[h // 8]
                hh = h % 8
                p, pb = PG[h % 2], PGb[h % 2]
                for kc in range(2):
                    OP(pe, lambda e: e.matmul(p[0:96, 0:TB], lhsT=wv[:, kc, 96 * hh:96 * hh + 96], rhs=latn[:, kc, :], start=(kc == 0), stop=(kc == 1)),
                       reads=[wbuf, bb("latn0"), bb("latn1")], writes=[pb])
                t1f, t1b = tf()
                t1 = t1f[:, :].bitcast(BF16)[:, 0:TB]
                OP(act, lambda e: e.activation(out=t1[0:96, :], in_=p[0:96, 0:TB], func=AF.Square), reads=[pb], writes=[t1b])
                qst[h] = (p, pb, t1, t1b)

            def q_s345(h):
                p, pb, t1, t1b = qst[h]
                p2, p2b = PG[2], PGb[2]
                OP(pe, lambda e: e.matmul(p2[0:96, 0:TB], lhsT=onesb[0:96, 0:96], rhs=t1[0:96, :], start=True, stop=True), reads=[bb("onesb"), t1b], writes=[p2b])
                t2, t2b = tf()
                rsqrt_big(t2, t2b, p2[0:96, 0:TB], p2b, 96.0, P=96)
                OP(dve, lambda e: e.scalar_tensor_tensor(out=R1[0:96, h, :], in0=p[0:96, 0:TB], scalar=cols[0:96, 39:40], in1=t2[0:96, :], op0=ALU.mult, op1=ALU.mult),
                   reads=[pb, COLS, t2b], writes=[bb(f"R1_{h}")])

            def q_s67(h):
                p3, p3b = PG[3], PGb[3]
                OP(pe, lambda e: e.matmul(p3[0:96, 0:TB], lhsT=permb[0:96, :], rhs=R1[0:96, h, :], start=True, stop=True), reads=[bb("permb"), bb(f"R1_{h}")], writes=[p3b])
                t3, t3b = tf()
                t4, t4b = tf()
                OP(dve, lambda e: e.tensor_tensor(out=t3[64:96, :], in0=p3[64:96, 0:TB], in1=sinT[64:96, :], op=ALU.mult), reads=[p3b, bb("sinT")], writes=[t3b])
                OP(pool, lambda e: e.tensor_tensor(out=t4[64:96, :], in0=R1[64:96, h, :], in1=cosT[64:96, :], op=ALU.mult), reads=[bb(f"R1_{h}"), bb("cosT")], writes=[t4b])
                OP(dve, lambda e: e.tensor_tensor(out=R1[64:96, h, :], in0=t3[64:96, :], in1=t4[64:96, :], op=ALU.add), reads=[t3b, t4b], writes=[bb(f"R1_{h}")])

            kv_units = []
            kvrot = {"n": 0}

            def kv_bank():
                k_ = kvrot["n"] % 2
                kvrot["n"] += 1
                return PG[4 + k_], PGb[4 + k_]

            def mk_kv0(tt):
                def f():
                    p, pb = kv_bank()
                    OP(pe, lambda e: e.matmul(p[:, 0:1], lhsT=kpq[64:96, tt * 128:(tt + 1) * 128], rhs=onesf[64:96, 0:1], start=True, stop=True),
                       reads=[bb("kpq"), CST], writes=[pb])
                    OP(act, lambda e: e.activation(out=ssk[:, 16:17], in_=p[:, 0:1], func=AF.Copy), reads=[pb], writes=[bb("ssk")])
                return f

            def mk_kvn(tt, n):
                def f():
                    kt = (t0 // 128) + tt
                    p, pb = kv_bank()
                    OP(pe, lambda e: e.matmul(p[:, :], lhsT=KC[:, t0 + tt * 128:t0 + (tt + 1) * 128], rhs=wukv_sb[:, 512 * n:512 * n + 512], start=True, stop=True),
                       reads=[bb("KC"), bb("wukv_sb")], writes=[pb])
                    pv = p[:, :].rearrange("p (h d) -> p h d", h=4)
                    OP(act, lambda e: e.activation(out=VA[:, kt, 4 * n:4 * n + 4, 0:64], in_=pv[:, :, 64:128], func=AF.Copy), reads=[pb], writes=[bb("VA")])
                    OP(act, lambda e: e.activation(out=sqk[:, :, :], in_=pv[:, :, 0:64], func=AF.Square), reads=[pb], writes=[bb("sqk")])
                    OP(dve, lambda e: e.tensor_reduce(out=ssk[:, 4 * n:4 * n + 4], in_=sqk[:, :, :], axis=AX.X, op=ALU.add), reads=[bb("sqk")], writes=[bb("ssk")])
                return f

            def mk_kvr(tt):
                def f():
                    kt = (t0 // 128) + tt
                    OP(dve, lambda e: e.tensor_scalar(out=rsk[:, kt, :], in0=ssk[:, 0:16], scalar1=ssk[:, 16:17], scalar2=1.0 / 96, op0=ALU.add, op1=ALU.mult),
                       reads=[bb("ssk")], writes=[bb("rsk")])
                    OP(dve, lambda e: e.tensor_scalar(out=rsk[:, kt, :], in0=rsk[:, kt, :], scalar1=EPS, scalar2=None, op0=ALU.add), reads=[bb("rsk")], writes=[bb("rsk")])
                    OP(act, lambda e: e.activation(out=rsk[:, kt, :], in_=rsk[:, kt, :], func=AF.Sqrt), reads=[bb("rsk")], writes=[bb("rsk")])
                    OP(dve, lambda e: e.reciprocal(out=rsk[:, kt, :], in_=rsk[:, kt, :]), reads=[bb("rsk")], writes=[bb("rsk")])
                return f

            for tt in range(NTT):
                kv_units.append(mk_kv0(tt))
                for n in range(4):
                    kv_units.append(mk_kvn(tt, n))
                kv_units.append(mk_kvr(tt))

            kvi = 0
            for it in range(-3, 16):
                if 0 <= it + 3 < 16:
                    q_s12(it + 3)
                if 0 <= it + 2 < 16:
                    q_s345(it + 2)
                if 0 <= it < 16:
                    q_s67(it)
                for _ in range(2 if it < 2 else 1):
                    if kvi < len(kv_units):
                        kv_units[kvi]()
                        kvi += 1
            while kvi < len(kv_units):
                kv_units[kvi]()
                kvi += 1
            rot["a"] = 0
            rot["g"] = 0
            if dbg_here:
                dump("d_qT", R1[0:96, 0:16, :].rearrange("p a b -> p (a b)"), RB(0, 16), 96, 16 * TB, BF16)
                dump("d_rsk", rsk[:, 0:NTT, :].rearrange("p a b -> p (a b)"), [bb("rsk")], 128, 16 * NTT, F32)
            srot = {"n": 0}

            def s_bank():
                i = srot["n"] % 4
                srot["n"] += 1
                return PG[i], PGb[i]

            def emit_qa(h):
                p, pb = PTF[0], PTb[0]
                OP(pe, lambda e: e.matmul(p[:, 0:TB], lhsT=WkgT[0:64, h, :], rhs=R1[0:64, h, :], start=True, stop=True), reads=[bb("WkgT"), bb(f"R1_{h}")], writes=[pb])
                qa, qab = qatile()
                OP(dve, lambda e: e.tensor_copy(out=qa[:, :], in_=p[:, 0:TB]), reads=[pb], writes=[qab])
                return qa, qab

            def geom(kt):
                i = kt - j * NTT
                c0 = 128 * i if i > 0 else 0
                return i, c0, TB - c0

            def emit_qk(h, kt, qa, qab):
                i, c0, w = geom(kt)
                p, pb = s_bank()
                OP(pe, lambda e: e.matmul(p[:, 0:w], lhsT=KC[:, kt * 128:(kt + 1) * 128], rhs=qa[:, c0:TB], start=True, stop=False),
                   reads=[bb("KC"), qab], writes=[pb])
                OP(pe, lambda e: e.matmul(p[:, 0:w], lhsT=KR[:, kt * 128:(kt + 1) * 128], rhs=R1[:, h, c0:TB], start=False, stop=True),
                   reads=[bb("KR"), bb(f"R1_{h}")], writes=[pb])
                return p, pb

            def emit_fin(h, oacc, oaccb):
                OP(dve, lambda e: e.tensor_copy(out=drow[64:65, :], in_=oacc[64:65, 0:TB]), reads=[oaccb], writes=[bb("drow")])
                p, pb = PTF[1], PTb[1]
                OP(pe, lambda e: e.matmul(p[0:64, 0:TB], lhsT=onesf[64:65, 0:64], rhs=drow[64:65, :], start=True, stop=True), reads=[CST, bb("drow")], writes=[pb])
                t1, t1b = tf()
                if j == 0:
                    OP(act, lambda e: e.activation(out=t1[0:64, :], in_=p[0:64, 0:TB], func=AF.Ln), reads=[pb], writes=[t1b])
                    OP(act, lambda e: e.activation(out=t1[0:64, :], in_=t1[0:64, :], func=AF.Exp, scale=-1.0), reads=[t1b], writes=[t1b])
                else:
                    OP(dve, lambda e: e.reciprocal(out=t1[0:64, :], in_=p[0:64, 0:TB]), reads=[pb], writes=[t1b])
                OP(dve, lambda e: e.tensor_tensor(out=R1[0:64, 16 + h, :], in0=oacc[0:64, 0:TB], in1=t1[0:64, :], op=ALU.mult), reads=[oaccb, t1b], writes=[bb(f"R1_{16 + h}")])

            steps = [(h, kt) for h in range(16) for kt in range(NK)]
            LA = 3
            qa_of = {0: emit_qa(0)}
            oacc_of = {}
            sq = {}

            def need_qa(hx):
                if hx < 16 and hx not in qa_of:
                    qa_of[hx] = emit_qa(hx)

            for k0 in range(min(LA, len(steps))):
                h2, kt2 = steps[k0]
                need_qa(h2)
                sq[k0] = emit_qk(h2, kt2, *qa_of[h2])
            pending_fin = None
            for si, (h, kt) in enumerate(steps):
                if kt == 0:
                    oacc_of[h] = pa()
                    need_qa(h + 1)
                p, pb = sq.pop(si)
                if si + LA < len(steps):
                    h2, kt2 = steps[si + LA]
                    need_qa(h2)
                    sq[si + LA] = emit_qk(h2, kt2, *qa_of[h2])
                i, c0, w = geom(kt)
                oacc, oaccb = oacc_of[h]
                pt_, ptb_ = ptile()
                OP(act, lambda e: e.activation(out=pt_[:, 0:w], in_=p[:, 0:w], func=AF.Exp, scale=rsk[:, kt, h:h + 1]), reads=[pb, bb("rsk")], writes=[ptb_])
                if i >= 0:
                    OP(act, lambda e: e.memzero(pt_[64:128, 0:64]), writes=[ptb_])
                va = VA[:, kt, h, 0:64]
                va128 = bass.AP(va.tensor, va.offset, [list(va.ap[0]), [64, 2], [1, 64]])
                OP(pe, lambda e: e.matmul(oacc[:, c0:TB], lhsT=va128, rhs=pt_[:, 0:w], start=(kt == 0), stop=(kt == NK - 1)),
                   reads=[bb("VA"), ptb_], writes=[oaccb])
                if pending_fin is not None and (kt >= 1 or NK == 1 or si == len(steps) - 1):
                    emit_fin(*pending_fin)
                    pending_fin = None
                if kt == NK - 1:
                    if pending_fin is not None:
                        emit_fin(*pending_fin)
                    pending_fin = (h, oacc, oaccb)
            if pending_fin is not None:
                emit_fin(*pending_fin)
            if dbg_here:
                dump("d_OT", R1[0:64, 16:32, :].rearrange("p a b -> p (a b)"), RB(16, 32), 64, 16 * TB, BF16)

            for cp in range(4):
                gts = {}
                for gi in range(2):
                    wv, wbuf = ws_get()
                    for i in range(2):
                        c = 2 * cp + i
                        p, pb = proj_fm(wv, wbuf, 128 * i, 128, lambda kc: hT[:, kc, :], [bb("hT")])
                        t, tb_ = tf()
                        OP(act, lambda e: e.activation(out=t[:, :], in_=p[:, 0:TB], func=AF.Sigmoid, bias=cols[:, 16 + 8 * gi + c:17 + 8 * gi + c]),
                           reads=[pb, COLS], writes=[tb_])
                        gts[(gi, i)] = (t, tb_)
                wv, wbuf = ws_get()
                for i in range(2):
                    p, pb = proj_fm(wv, wbuf, 128 * i, 128, lambda kc: ogT[:, kc, :], [bb("ogT")])
                    t, tb_ = gts[(0, i)]
                    OP(dve, lambda e: e.tensor_tensor(out=t[:, :], in0=p[:, 0:TB], in1=t[:, :], op=ALU.mult), reads=[pb, tb_], writes=[tb_])
                for i in range(2):
                    c = 2 * cp + i
                    wv, wbuf = ws_get()
                    p, pb = pg()
                    for h in range(16):
                        OP(pe, lambda e: e.matmul(p[:, 0:TB], lhsT=wv[0:64, h, :], rhs=R1[0:64, 16 + h, :], start=(h == 0), stop=(h == 15)),
                           reads=[wbuf, bb(f"R1_{16 + h}")], writes=[pb])
                    t, tb_ = gts[(1, i)]
                    OP(dve, lambda e: e.tensor_tensor(out=t[:, :], in0=p[:, 0:TB], in1=t[:, :], op=ALU.mult), reads=[pb, tb_], writes=[tb_])
                    ta, tab = gts[(0, i)]
                    OP(pool, lambda e: e.tensor_tensor(out=R1[:, c, :], in0=ta[:, :], in1=t[:, :], op=ALU.add), reads=[tab, tb_], writes=[bb(f"R1_{c}")])
            if dbg_here:
                dump("d_mixT", R1[:, 0:8, :].rearrange("p a b -> p (a b)"), RB(0, 8), 128, 8 * TB, BF16)

            for n in range(4):
                wv, wbuf = ws_get()
                for tt in range(NTT):
                    p, pb = pg()
                    for kc in range(8):
                        OP(pe, lambda e: e.matmul(p[:, 0:256], lhsT=R1[:, kc, tt * 128:(tt + 1) * 128], rhs=wv[:, kc, :], start=(kc == 0), stop=(kc == 7)),
                           reads=[wbuf, bb(f"R1_{kc}")], writes=[pb])
                    t, tb_ = tf()
                    c0 = 256 * n
                    OP(dve, lambda e: e.tensor_tensor(out=t[:, 0:256], in0=p[:, 0:256], in1=gate_bc[:, 0, c0:c0 + 256], op=ALU.mult), reads=[pb, bb("gate_bc")], writes=[tb_])
                    OP(pool, lambda e: e.tensor_tensor(out=xt[:, tt, c0:c0 + 256], in0=xt[:, tt, c0:c0 + 256], in1=t[:, 0:256], op=ALU.add), reads=[tb_, bb(f"xt{tt}")], writes=[bb(f"xt{tt}")])
            if dbg_here:
                dump("d_x1", xt[:].rearrange("p a b -> p (a b)"), [bb(f"xt{t}") for t in range(NTT)], 128, NTT * D, F32)

            norm_to_hT(b, 1)
            for n in range(16):
                wv, wbuf = ws_get()
                for i in range(2):
                    f = 2 * n + i
                    p, pb = proj_fm(wv, wbuf, 128 * i, 128, lambda kc: hT[:, kc, :], [bb("hT")])
                    t, tb_ = tf()
                    OP(act, lambda e: e.activation(out=t[:, :], in_=p[:, 0:TB], func=AF.Relu), reads=[pb], writes=[tb_])
                    OP(pool, lambda e: e.tensor_tensor(out=R1[:, f, :], in0=t[:, :], in1=t[:, :], op=ALU.mult), reads=[tb_], writes=[bb(f"R1_{f}")])
            accs = [(PG[k], PGb[k]) for k in range(4)]
            for half in range(2):
                for n in range(8):
                    wv, wbuf = ws_get()
                    for tt in range(NTT):
                        p, pb = accs[tt]
                        for i in range(4):
                            f = 4 * n + i
                            OP(pe, lambda e: e.matmul(p[:, :], lhsT=R1[:, f, tt * 128:(tt + 1) * 128], rhs=wv[:, i, :],
                                                      start=(n == 0 and i == 0), stop=(n == 7 and i == 3)),
                               reads=[wbuf, bb(f"R1_{f}")], writes=[pb])
                for tt in range(NTT):
                    p, pb = accs[tt]
                    t, tb_ = tf()
                    c0 = 512 * half
                    OP(dve, lambda e: e.tensor_tensor(out=t[:, :], in0=p[:, :], in1=gate_bc[:, 1, c0:c0 + 512], op=ALU.mult), reads=[pb, bb("gate_bc")], writes=[tb_])
                    OP(pool, lambda e: e.tensor_tensor(out=xt[:, tt, c0:c0 + 512], in0=xt[:, tt, c0:c0 + 512], in1=t[:, :], op=ALU.add), reads=[tb_, bb(f"xt{tt}")], writes=[bb(f"xt{tt}")])
            for tt in range(NTT):
                fw.dma(sp, xst[tt], out_d[b, t0 + tt * 128:t0 + (tt + 1) * 128, :], xt[:, tt, :], reads=[bb(f"xt{tt}")])
            rot["g"] = 0

        dumpt = {}

        def dump(name, src, bufs, P, n, dt, three=False):
            if dt == F32:
                fw.dma(sp, dbgs, dbg_d[name], src, reads=bufs)
                return
            if "dmp" not in dumpt:
                dumpt["dmp"] = (fw.sbuf("dmp", [128, TB], F32), Buf("dmp"))
            t, tb_ = dumpt["dmp"]
            for a in range(n // TB):
                piece = src[:, a, :] if three else src[:, a * TB:(a + 1) * TB]
                OP(dve, lambda e: e.tensor_copy(out=t[0:P, :], in_=piece), reads=bufs, writes=[tb_])
                fw.dma(sp, dbgs, dbg_d[name][:, a * TB:(a + 1) * TB], t[0:P, :], reads=[tb_])

        nblk = max_blocks if max_blocks else NBLK
        for b in range(nseq):
            compute_gates(b)
            for h in range(4):
                OP(pool, lambda e: e.memset(S[:, h, :], 0.0), writes=[bb(f"S{h}")])
            for j in range(nblk):
                do_block(b, j)
        for ds in xst + ([dbgs] if debug else []):
            if ds[1] > 0:
                sp.e.wait_ge(ds[0], ds[1])
    return nc


_CACHE = {}


def make_in_maps(inputs, ncores=8):
    f = lambda a: np.ascontiguousarray(np.asarray(a))
    cstv = host_consts()
    shared = {
        "w_ada": f(inputs["w_ada"][0]), "b_ada": f(inputs["b_ada"]), "norm1_g": f(inputs["norm1_g"]),
        "w_in": f(inputs["w_in"][0]), "b_merge": f(inputs["b_merge"]), "gla_w_alpha": f(inputs["gla_w_alpha"][0]),
        "gla_b_alpha": f(inputs["gla_b_alpha"]), "gla_out_norm_g": f(inputs["gla_out_norm_g"]),
        "gla_w_o": f(inputs["gla_w_o"][0]), "mla_q_lat_g": f(inputs["mla_q_lat_g"]), "mla_w_uq": f(inputs["mla_w_uq"][0]),
        "mla_kv_lat_g": f(inputs["mla_kv_lat_g"]), "mla_w_ukv": f(inputs["mla_w_ukv"][0]), "mla_qn_g": f(inputs["mla_qn_g"]),
        "mla_kn_g": f(inputs["mla_kn_g"]), "mla_w_o": f(inputs["mla_w_o"][0]), "w_out": f(inputs["w_out"][0]),
        "norm2_g": f(inputs["norm2_g"]), "mlp_w1": f(inputs["mlp_w1"][0]), "mlp_w2": f(inputs["mlp_w2"][0]), "cst": cstv,
    }
    maps = []
    for c in range(ncores):
        m = dict(shared)
        m["x"] = f(inputs["x"][NB * c:NB * (c + 1)])
        m["c"] = f(inputs["c"][NB * c:NB * (c + 1)])
        m["positions"] = f(inputs["positions"][NB * c:NB * (c + 1)]).astype(np.int32)
        maps.append(m)
    return maps


def kernel(**inputs):
    if "nc" not in _CACHE:
        _CACHE["nc"] = build()
    nc = _CACHE["nc"]
    maps = make_in_maps(inputs)
    res = run_bass_kernel_spmd(nc, maps, core_ids=list(range(8)))
    out = np.concatenate([np.asarray(r["out"]) for r in res.results], axis=0)
    return out.astype(np.float32)
```
